# Optimizing a Trainium2 kernel written in Bass

```python
import functools, math
import jax, jax.numpy as jnp
from jax import lax
import numpy as np

D_MODEL = 1024
BATCH = 4
SEQ = 4096
DEPTH = 4
DEC_BATCH = 32
DEC_SEQ = 8
PAST_LEN = 8192
PAGE_SIZE = 128

N_MIXERS = 4
EXPAND = 2
D_INNER = EXPAND * D_MODEL
A_HEADS = 4
A_HEAD_DIM = D_INNER // A_HEADS
A_CONV = 4
A_CHUNK = 64
B_GROUP = 16
B_GROUPS = D_INNER // B_GROUP
B_STATE = 64
B_CHUNK = 128
B_DT_MIN = 1e-3
B_DT_MAX = 1e-1
C_HEADS = 8
C_QK_DIM = 64
C_V_DIM = D_INNER // C_HEADS
C_PATTERNS = ((128, 1), (512, 4), (2048, 16))
C_MAX_WINDOW = max(w for w, _ in C_PATTERNS)
C_BLOCK = 128
D_WINDOWS = (2, 4, 8, 16)
D_GROUP = D_INNER // len(D_WINDOWS)
D_PREFIX = max(D_WINDOWS) - 1
RMS_EPS = 1e-6
HEAD_NORM_EPS = 1e-6
F32 = jnp.float32

kernel_name = 'hybrid_mlstm_s5_dilated_pool_decoder_step'


def rmsnorm(x, g):
    xf = x.astype(F32)
    y = xf * lax.rsqrt(jnp.mean(xf * xf, axis=-1, keepdims=True) + RMS_EPS)
    return (y * g.astype(F32)).astype(x.dtype)


def cmul(ar, ai, br, bi):
    return ar * br - ai * bi, ar * bi + ai * br


def causal_depthwise_conv(u, prev, w, b):
    width, L = w.shape[0], u.shape[1]
    ext = jnp.concatenate([prev.astype(u.dtype), u], axis=1)
    out = b + sum(ext[:, i:i + L] * w[i] for i in range(width))
    return out, ext[:, L:]


def mlstm_cell(q, k, v, ig, lf, c0, n0, m0):
    bsz, L, H, Dh = q.shape
    lc = A_CHUNK if L % A_CHUNK == 0 else L
    nc = L // lc

    def to_chunks(a):
        return jnp.moveaxis(a.reshape((bsz, nc, lc) + a.shape[2:]), 1, 0)

    causal = jnp.tril(jnp.ones((lc, lc), dtype=bool))

    def step(carry, inp):
        c, n, m = carry
        qc, kc, vc, ic, fc = inp
        F = jnp.cumsum(fc, axis=1)
        dlog = F[:, :, None, :] - F[:, None, :, :] + ic[:, None, :, :]
        dlog = jnp.where(causal[None, :, :, None], dlog, -jnp.inf)
        inter = F + m[:, None, :]
        mt = jnp.maximum(inter, jnp.max(dlog, axis=2))
        w = jnp.exp(dlog - mt[:, :, None, :])
        a = jnp.exp(inter - mt)
        s = jnp.einsum('bthd,bshd->btsh', qc, kc) * w
        num = a[..., None] * jnp.einsum('bthd,bhde->bthe', qc, c) + jnp.einsum('btsh,bshe->bthe', s, vc)
        den = a * jnp.einsum('bthd,bhd->bth', qc, n) + jnp.sum(s, axis=2)
        h = num / jnp.maximum(jnp.abs(den), jnp.exp(-mt))[..., None]
        m_new = mt[:, -1]
        f_tot = F[:, -1]
        decay = jnp.exp(f_tot + m - m_new)
        ws = jnp.exp(f_tot[:, None] - F + ic - m_new[:, None])
        c_new = decay[..., None, None] * c + jnp.einsum('bsh,bshd,bshe->bhde', ws, kc, vc)
        n_new = decay[..., None] * n + jnp.einsum('bsh,bshd->bhd', ws, kc)
        return (c_new, n_new, m_new), h

    xs = (to_chunks(q), to_chunks(k), to_chunks(v), to_chunks(ig), to_chunks(lf))
    (c, n, m), hs = lax.scan(step, (c0, n0, m0), xs)
    h = jnp.moveaxis(hs, 0, 1).reshape(bsz, L, H, Dh)
    return h, c, n, m


def mlstm_mixer(h, conv_prev, c0, n0, m0, w_in, b_gate, conv_w, conv_b, w_q, w_k, w_v, norm_g, skip, w_out):
    bsz, L, _ = h.shape
    E, H, Dh = D_INNER, A_HEADS, A_HEAD_DIM
    xm, z, o_pre, gates = jnp.split(h @ w_in, [E, 2 * E, 3 * E], axis=-1)
    xconv, conv_state = causal_depthwise_conv(xm, conv_prev, conv_w, conv_b)
    xc = jax.nn.silu(xconv)
    xc_h = xc.astype(F32).reshape(bsz, L, H, Dh)
    xm_h = xm.astype(F32).reshape(bsz, L, H, Dh)
    q = jnp.einsum('blhd,hde->blhe', xc_h, w_q.astype(F32))
    k = jnp.einsum('blhd,hde->blhe', xc_h, w_k.astype(F32)) * (Dh ** -0.5)
    v = jnp.einsum('blhd,hde->blhe', xm_h, w_v.astype(F32))
    gates = gates.astype(F32) + b_gate.astype(F32)
    ig, lf = gates[..., :H], jax.nn.log_sigmoid(gates[..., H:])
    hc, c, n, m = mlstm_cell(q, k, v, ig, lf, c0.astype(F32), n0.astype(F32), m0.astype(F32))
    hc = hc * jax.nn.sigmoid(o_pre.astype(F32)).reshape(bsz, L, H, Dh)
    mu = jnp.mean(hc, axis=-1, keepdims=True)
    var = jnp.mean(jnp.square(hc - mu), axis=-1, keepdims=True)
    hn = ((hc - mu) * lax.rsqrt(var + HEAD_NORM_EPS)).reshape(bsz, L, E)
    hn = hn * norm_g.astype(F32) + skip.astype(F32) * xc.astype(F32)
    out = (hn.astype(h.dtype) * jax.nn.silu(z)) @ w_out
    return out, conv_state, c, n, m


def s5_discretise(lam_re, lam_im, log_dt, b_re, b_im):
    lr = jnp.minimum(lam_re, -1e-4)
    li = lam_im
    dt = jnp.exp(log_dt)[:, None]
    mag = jnp.exp(dt * lr)
    a_re, a_im = mag * jnp.cos(dt * li), mag * jnp.sin(dt * li)
    den = lr * lr + li * li
    xr, xi = a_re - 1.0, a_im
    coef_re = (xr * lr + xi * li) / den
    coef_im = (xi * lr - xr * li) / den
    bb_re, bb_im = cmul(coef_re[..., None], coef_im[..., None], b_re, b_im)
    return a_re, a_im, bb_re, bb_im


def s5_scan(ug, bb_re, bb_im, a_re, a_im, c_re, c_im, h0_re, h0_im):
    bsz, L, G, _ = ug.shape
    lc = B_CHUNK if L % B_CHUNK == 0 else L
    nc = L // lc
    xs = jnp.moveaxis(ug.reshape((bsz, nc, lc) + ug.shape[2:]), 1, 0)

    def combine(e1, e2):
        a1r, a1i, b1r, b1i = e1
        a2r, a2i, b2r, b2i = e2
        ar, ai = cmul(a2r, a2i, a1r, a1i)
        br, bi = cmul(a2r, a2i, b1r, b1i)
        return ar, ai, br + b2r, bi + b2i

    def step(carry, u_c):
        hr, hi = carry
        bu_r = jnp.einsum('blgc,gpc->blgp', u_c, bb_re)
        bu_i = jnp.einsum('blgc,gpc->blgp', u_c, bb_im)
        ar = jnp.broadcast_to(a_re, bu_r.shape)
        ai = jnp.broadcast_to(a_im, bu_r.shape)
        pr, pim, sr, si = lax.associative_scan(combine, (ar, ai, bu_r, bu_i), axis=1)
        cr, ci = cmul(pr, pim, hr[:, None], hi[:, None])
        st_r, st_i = sr + cr, si + ci
        y = jnp.einsum('blgp,gcp->blgc', st_r, c_re) - jnp.einsum('blgp,gcp->blgc', st_i, c_im)
        return (st_r[:, -1], st_i[:, -1]), y

    (hr, hi), ys = lax.scan(step, (h0_re, h0_im), xs)
    y = jnp.moveaxis(ys, 0, 1).reshape(bsz, L, G * B_GROUP)
    return y, hr, hi


def s5_mixer(h, h0_re, h0_im, w_in, lam_re, lam_im, log_dt, b_re, b_im, c_re, c_im, d_skip, w_glu, w_out):
    bsz, L, _ = h.shape
    u, z = jnp.split(h @ w_in, 2, axis=-1)
    uf = u.astype(F32)
    a_re, a_im, bb_re, bb_im = s5_discretise(lam_re.astype(F32), lam_im.astype(F32), log_dt.astype(F32),
                                             b_re.astype(F32), b_im.astype(F32))
    ug = uf.reshape(bsz, L, B_GROUPS, B_GROUP)
    y, hr, hi = s5_scan(ug, bb_re, bb_im, a_re, a_im, c_re.astype(F32), c_im.astype(F32),
                        h0_re.astype(F32), h0_im.astype(F32))
    y = y + d_skip.astype(F32) * uf
    g = jax.nn.gelu(y)
    g = g * jax.nn.sigmoid(g @ w_glu.astype(F32))
    out = (g.astype(h.dtype) * jax.nn.silu(z)) @ w_out
    return out, hr, hi


def residue_split(t, d):
    bsz, S = t.shape[:2]
    t = jnp.swapaxes(t.reshape((bsz, S // d, d) + t.shape[2:]), 1, 2)
    return t.reshape((bsz * d, S // d) + t.shape[3:])


def residue_merge(t, bsz, d):
    n = t.shape[1]
    t = jnp.swapaxes(t.reshape((bsz, d, n) + t.shape[2:]), 1, 2)
    return t.reshape((bsz, n * d) + t.shape[3:])


def band_attention(q, k, v, reach):
    N, n, H, dk = q.shape
    dv = v.shape[-1]
    Q = C_BLOCK
    n_pad = -(-n // Q) * Q
    nb = n_pad // Q
    pad = lambda t: jnp.pad(t, ((0, 0), (0, n_pad - n), (0, 0), (0, 0)))
    qb = pad(q).astype(F32).reshape(N, nb, Q, H, dk)
    kb = pad(k).astype(F32).reshape(N, nb, Q, H, dk)
    vb = pad(v).astype(F32).reshape(N, nb, Q, H, dv)
    with_prev = lambda t: jnp.concatenate(
        [jnp.concatenate([jnp.zeros_like(t[:, :1]), t[:, :-1]], axis=1), t], axis=2)
    kk, vv = with_prev(kb), with_prev(vb)
    s = jnp.einsum('nbqhd,nbkhd->nbhqk', qb, kk)
    qpos = jnp.arange(nb)[:, None, None] * Q + jnp.arange(Q)[None, :, None]
    kpos = jnp.arange(nb)[:, None, None] * Q - Q + jnp.arange(2 * Q)[None, None, :]
    dist = qpos - kpos
    mask = (dist >= 0) & (dist <= reach) & (kpos >= 0)
    s = jnp.where(mask[None, :, None], s, -jnp.inf)
    m = jnp.max(s, axis=-1)
    p = jnp.exp(s - m[..., None])
    den = jnp.sum(p, axis=-1)
    num = jnp.einsum('nbhqk,nbkhe->nbqhe', p, vv)
    unblock = lambda t: t.reshape((N, n_pad) + t.shape[3:])[:, :n]
    return unblock(num), unblock(jnp.swapaxes(m, 2, 3)), unblock(jnp.swapaxes(den, 2, 3))


def merge_by_denominator(parts):
    mmax = functools.reduce(jnp.maximum, [m for _, m, _ in parts])
    num = sum(nu * jnp.exp(m - mmax)[..., None] for nu, m, _ in parts)
    den = sum(de * jnp.exp(m - mmax) for _, m, de in parts)
    return num / den[..., None]


def dilated_project(h, w_in):
    bsz, L = h.shape[:2]
    nq = C_HEADS * C_QK_DIM
    q, k, v, z = jnp.split(h @ w_in, [3 * nq, 6 * nq, 6 * nq + C_HEADS * C_V_DIM], axis=-1)
    q = q.reshape(bsz, L, len(C_PATTERNS), C_HEADS, C_QK_DIM) * (C_QK_DIM ** -0.5)
    k = k.reshape(bsz, L, len(C_PATTERNS), C_HEADS, C_QK_DIM)
    v = v.reshape(bsz, L, C_HEADS, C_V_DIM)
    return q, k, v, z


def dilated_prompt(h, w_in, w_out):
    bsz, L = h.shape[:2]
    q, k, v, z = dilated_project(h, w_in)
    parts = []
    for g, (win, dil) in enumerate(C_PATTERNS):
        nu, m, de = band_attention(residue_split(q[:, :, g], dil), residue_split(k[:, :, g], dil),
                                   residue_split(v, dil), win // dil)
        parts.append((residue_merge(nu, bsz, dil), residue_merge(m, bsz, dil), residue_merge(de, bsz, dil)))
    o = merge_by_denominator(parts).reshape(bsz, L, D_INNER)
    out = (o.astype(h.dtype) * jax.nn.silu(z)) @ w_out
    k_bufs = tuple(k[:, L - min(win, L):, g] for g, (win, _) in enumerate(C_PATTERNS))
    v_buf = v[:, L - min(C_MAX_WINDOW, L):]
    return out, k_bufs, v_buf


def dilated_sample(h, k_caches, v_cache, w_in, w_out):
    bsz, L = h.shape[:2]
    q, k, v, z = dilated_project(h, w_in)
    lv = v_cache.shape[1]
    vc = jnp.concatenate([v_cache.astype(v.dtype), v], axis=1)
    i = jnp.arange(L)[:, None]
    parts, new_k = [], []
    for g, (win, dil) in enumerate(C_PATTERNS):
        lk = k_caches[g].shape[1]
        kc = jnp.concatenate([k_caches[g].astype(k.dtype), k[:, :, g]], axis=1)
        back = i - jnp.arange(win // dil + 1)[None, :] * dil
        valid = (lk + back) >= 0
        kg = kc[:, jnp.maximum(lk + back, 0)].astype(F32)
        vg = vc[:, jnp.maximum(lv + back, 0)].astype(F32)
        s = jnp.einsum('bihd,bijhd->bihj', q[:, :, g].astype(F32), kg)
        s = jnp.where(valid[None, :, None, :], s, -jnp.inf)
        m = jnp.max(s, axis=-1)
        p = jnp.exp(s - m[..., None])
        parts.append((jnp.einsum('bihj,bijhe->bihe', p, vg), m, jnp.sum(p, axis=-1)))
        new_k.append(kc[:, L:])
    o = merge_by_denominator(parts).reshape(bsz, L, D_INNER)
    out = (o.astype(h.dtype) * jax.nn.silu(z)) @ w_out
    return out, tuple(new_k), vc[:, L:]


def pool_mixer(h, prefix, start, w_in, w_grp, scale, w_out):
    bsz, L, _ = h.shape
    u, z = jnp.split(h @ w_in, 2, axis=-1)
    uf = u.astype(F32)
    uc = jnp.concatenate([prefix.astype(F32), uf], axis=1)
    cs = jnp.concatenate([jnp.zeros((bsz, 1, D_INNER), F32), jnp.cumsum(uc, axis=1)], axis=1)
    pos = start + jnp.arange(L)
    pooled = []
    for g, w in enumerate(D_WINDOWS):
        lo, hi = g * D_GROUP, (g + 1) * D_GROUP
        tot = cs[:, D_PREFIX + 1:D_PREFIX + 1 + L, lo:hi] - cs[:, D_PREFIX + 1 - w:D_PREFIX + 1 - w + L, lo:hi]
        cnt = jnp.minimum(pos + 1, w).astype(F32)
        pooled.append(tot / cnt[None, :, None])
    mix = jnp.stack(pooled, axis=2) - uf.reshape(bsz, L, len(D_WINDOWS), D_GROUP)
    mix = jnp.einsum('blgc,gcd->blgd', mix, w_grp.astype(F32)).reshape(bsz, L, D_INNER) * scale.astype(F32)
    out = (mix.astype(h.dtype) * jax.nn.silu(z)) @ w_out
    return out, uc[:, L:]


def setup_inputs(seed: int = 0) -> dict:
    key = jax.random.key(seed)
    ks = iter(jax.random.split(key, 64))
    nrm = lambda shape, sc: jax.random.normal(next(ks), shape, F32) * sc
    NA, NB, NC, ND = (len(range(kd, DEPTH, N_MIXERS)) for kd in range(N_MIXERS))
    E, H, Dh = D_INNER, A_HEADS, A_HEAD_DIM
    lk = [min(w, PAST_LEN) for w, _ in C_PATTERNS]
    lv = min(C_MAX_WINDOW, PAST_LEN)
    f_bias = jnp.linspace(A_FGATE_LO, A_FGATE_HI, H)[None] if False else jnp.linspace(3.0, 6.0, H)[None]
    return {
        'x_prompt': nrm((BATCH, SEQ, D_MODEL), 1.0),
        'x_sample': nrm((DEC_BATCH, DEC_SEQ, D_MODEL), 1.0),
        'state_mlstm_c': nrm((NA, DEC_BATCH, H, Dh, Dh), 0.02),
        'state_mlstm_n': nrm((NA, DEC_BATCH, H, Dh), 0.1),
        'state_mlstm_m': nrm((NA, DEC_BATCH, H), 1.0),
        'state_mlstm_conv': nrm((NA, DEC_BATCH, A_CONV - 1, E), 1.0),
        'state_s5_re': nrm((NB, DEC_BATCH, B_GROUPS, B_STATE), 0.1),
        'state_s5_im': nrm((NB, DEC_BATCH, B_GROUPS, B_STATE), 0.1),
        'cache_dil_k1': nrm((NC, DEC_BATCH, lk[0], C_HEADS, C_QK_DIM), 1.0),
        'cache_dil_k2': nrm((NC, DEC_BATCH, lk[1], C_HEADS, C_QK_DIM), 1.0),
        'cache_dil_k3': nrm((NC, DEC_BATCH, lk[2], C_HEADS, C_QK_DIM), 1.0),
        'cache_dil_v': nrm((NC, DEC_BATCH, lv, C_HEADS, C_V_DIM), 1.0),
        'state_pool': nrm((ND, DEC_BATCH, D_PREFIX, E), 1.0),
        'norm_g': 1.0 + nrm((DEPTH, D_MODEL), 0.02),
        'final_norm_g': 1.0 + nrm((D_MODEL,), 0.02),
        'a_w_in': nrm((NA, D_MODEL, 3 * E + 2 * H), D_MODEL ** -0.5),
        'a_b_gate': jnp.concatenate([nrm((NA, H), 0.1), f_bias + nrm((NA, H), 0.1)], axis=-1),
        'a_conv_w': nrm((NA, A_CONV, E), A_CONV ** -0.5),
        'a_conv_b': nrm((NA, E), 0.02),
        'a_w_q': nrm((NA, H, Dh, Dh), Dh ** -0.5),
        'a_w_k': nrm((NA, H, Dh, Dh), Dh ** -0.5),
        'a_w_v': nrm((NA, H, Dh, Dh), Dh ** -0.5),
        'a_norm_g': 1.0 + nrm((NA, E), 0.02),
        'a_skip': 1.0 + nrm((NA, E), 0.02),
        'a_w_out': nrm((NA, E, D_MODEL), E ** -0.5),
        'b_w_in': nrm((NB, D_MODEL, 2 * E), D_MODEL ** -0.5),
        'b_lam_re': -0.5 + nrm((NB, B_GROUPS, B_STATE), 0.01),
        'b_lam_im': jnp.pi * jnp.arange(B_STATE, dtype=F32) + nrm((NB, B_GROUPS, B_STATE), 0.01),
        'b_log_dt': jax.random.uniform(next(ks), (NB, B_GROUPS), F32, math.log(B_DT_MIN), math.log(B_DT_MAX)),
        'b_B_re': nrm((NB, B_GROUPS, B_STATE, B_GROUP), (2 * B_GROUP) ** -0.5),
        'b_B_im': nrm((NB, B_GROUPS, B_STATE, B_GROUP), (2 * B_GROUP) ** -0.5),
        'b_C_re': nrm((NB, B_GROUPS, B_GROUP, B_STATE), B_STATE ** -0.5),
        'b_C_im': nrm((NB, B_GROUPS, B_GROUP, B_STATE), B_STATE ** -0.5),
        'b_d': nrm((NB, E), 1.0),
        'b_w_glu': nrm((NB, E, E), E ** -0.5),
        'b_w_out': nrm((NB, E, D_MODEL), E ** -0.5),
        'c_w_in': nrm((NC, D_MODEL, 6 * C_HEADS * C_QK_DIM + C_HEADS * C_V_DIM + E), D_MODEL ** -0.5),
        'c_w_out': nrm((NC, E, D_MODEL), E ** -0.5),
        'd_w_in': nrm((ND, D_MODEL, 2 * E), D_MODEL ** -0.5),
        'd_w_grp': nrm((ND, len(D_WINDOWS), D_GROUP, D_GROUP), D_GROUP ** -0.5),
        'd_scale': 1.0 + nrm((ND, E), 0.1),
        'd_w_out': nrm((ND, E, D_MODEL), E ** -0.5),
    }


def reference(x_prompt, x_sample, state_mlstm_c, state_mlstm_n, state_mlstm_m, state_mlstm_conv,
              state_s5_re, state_s5_im, cache_dil_k1, cache_dil_k2, cache_dil_k3, cache_dil_v, state_pool,
              norm_g, final_norm_g,
              a_w_in, a_b_gate, a_conv_w, a_conv_b, a_w_q, a_w_k, a_w_v, a_norm_g, a_skip, a_w_out,
              b_w_in, b_lam_re, b_lam_im, b_log_dt, b_B_re, b_B_im, b_C_re, b_C_im, b_d, b_w_glu, b_w_out,
              c_w_in, c_w_out, d_w_in, d_w_grp, d_scale, d_w_out):
    bp, xdt = x_prompt.shape[0], x_prompt.dtype
    names = ('mlstm_c', 'mlstm_n', 'mlstm_m', 'mlstm_conv', 's5_re', 's5_im', 'k1', 'k2', 'k3', 'v', 'pool')
    new_p = {nm: [] for nm in names}
    new_s = {nm: [] for nm in names}
    yp, ys = x_prompt, x_sample
    for layer in range(DEPTH):
        kind, j = layer % N_MIXERS, layer // N_MIXERS
        hp, hs = rmsnorm(yp, norm_g[layer]), rmsnorm(ys, norm_g[layer])
        if kind == 0:
            w = (a_w_in[j], a_b_gate[j], a_conv_w[j], a_conv_b[j], a_w_q[j], a_w_k[j], a_w_v[j],
                 a_norm_g[j], a_skip[j], a_w_out[j])
            op, *sp = mlstm_mixer(hp, jnp.zeros((bp, A_CONV - 1, D_INNER), xdt),
                                  jnp.zeros((bp, A_HEADS, A_HEAD_DIM, A_HEAD_DIM), F32),
                                  jnp.zeros((bp, A_HEADS, A_HEAD_DIM), F32), jnp.zeros((bp, A_HEADS), F32), *w)
            os_, *ss = mlstm_mixer(hs, state_mlstm_conv[j], state_mlstm_c[j], state_mlstm_n[j], state_mlstm_m[j], *w)
            keys = ('mlstm_conv', 'mlstm_c', 'mlstm_n', 'mlstm_m')
        elif kind == 1:
            w = (b_w_in[j], b_lam_re[j], b_lam_im[j], b_log_dt[j], b_B_re[j], b_B_im[j], b_C_re[j], b_C_im[j],
                 b_d[j], b_w_glu[j], b_w_out[j])
            zs = jnp.zeros((bp, B_GROUPS, B_STATE), F32)
            op, *sp = s5_mixer(hp, zs, zs, *w)
            os_, *ss = s5_mixer(hs, state_s5_re[j], state_s5_im[j], *w)
            keys = ('s5_re', 's5_im')
        elif kind == 2:
            op, kp, vp = dilated_prompt(hp, c_w_in[j], c_w_out[j])
            os_, kq, vq = dilated_sample(hs, (cache_dil_k1[j], cache_dil_k2[j], cache_dil_k3[j]), cache_dil_v[j],
                                         c_w_in[j], c_w_out[j])
            sp, ss = (*kp, vp), (*kq, vq)
            keys = ('k1', 'k2', 'k3', 'v')
        else:
            w = (d_w_in[j], d_w_grp[j], d_scale[j], d_w_out[j])
            op, *sp = pool_mixer(hp, jnp.zeros((bp, D_PREFIX, D_INNER), xdt), 0, *w)
            os_, *ss = pool_mixer(hs, state_pool[j], PAST_LEN, *w)
            keys = ('pool',)
        for nm, a, b in zip(keys, sp, ss):
            new_p[nm].append(a)
            new_s[nm].append(b)
        yp, ys = yp + op, ys + os_
    y_prompt = rmsnorm(yp, final_norm_g)
    y_sample = rmsnorm(ys, final_norm_g)
    mlstm_c_p, mlstm_c_s = jnp.stack(new_p['mlstm_c']), jnp.stack(new_s['mlstm_c'])
    mlstm_n_p, mlstm_n_s = jnp.stack(new_p['mlstm_n']), jnp.stack(new_s['mlstm_n'])
    mlstm_m_p, mlstm_m_s = jnp.stack(new_p['mlstm_m']), jnp.stack(new_s['mlstm_m'])
    mlstm_conv_p, mlstm_conv_s = jnp.stack(new_p['mlstm_conv']), jnp.stack(new_s['mlstm_conv'])
    s5_re_p, s5_re_s = jnp.stack(new_p['s5_re']), jnp.stack(new_s['s5_re'])
    s5_im_p, s5_im_s = jnp.stack(new_p['s5_im']), jnp.stack(new_s['s5_im'])
    k1_p, k1_s = jnp.stack(new_p['k1']), jnp.stack(new_s['k1'])
    k2_p, k2_s = jnp.stack(new_p['k2']), jnp.stack(new_s['k2'])
    k3_p, k3_s = jnp.stack(new_p['k3']), jnp.stack(new_s['k3'])
    v_p, v_s = jnp.stack(new_p['v']), jnp.stack(new_s['v'])
    pool_p, pool_s = jnp.stack(new_p['pool']), jnp.stack(new_s['pool'])
    return (y_prompt, y_sample, mlstm_c_p, mlstm_c_s, mlstm_n_p, mlstm_n_s, mlstm_m_p, mlstm_m_s,
            mlstm_conv_p, mlstm_conv_s, s5_re_p, s5_re_s, s5_im_p, s5_im_s, k1_p, k1_s, k2_p, k2_s,
            k3_p, k3_s, v_p, v_s, pool_p, pool_s)
```

```python
import math
from contextlib import ExitStack, contextmanager
import numpy as np
import concourse.bass as bass
import concourse.mybir as mybir
from concourse.bass_utils import run_bass_kernel_spmd

F32 = mybir.dt.float32
BF16 = mybir.dt.bfloat16
AF = mybir.ActivationFunctionType
ALU = mybir.AluOpType
AX = mybir.AxisListType

D = 1024
E = 2048
NSMP = 4
LS = 8
NST = NSMP * LS
PAST = 8192
RMS_EPS = 1e-6
NEG = -30000.0


class Trk:
    __slots__ = ("name", "writers", "readers", "excl")

    def __init__(self, name="", excl=False):
        self.name = name
        self.writers = []
        self.readers = []
        self.excl = excl


class V:
    __slots__ = ("ap", "trk")

    def __init__(self, ap, trk):
        self.ap = ap
        self.trk = trk

    def __getitem__(self, idx):
        return V(self.ap[idx], self.trk)

    def re(self, pat, **kw):
        return V(self.ap.rearrange(pat, **kw), self.trk)

    def bc(self, shape):
        return V(self.ap.to_broadcast(shape), self.trk)

    def bitcast(self, dt):
        return V(self.ap.bitcast(dt), self.trk)

    def sub(self, trk):
        return V(self.ap, trk)


def _ap(x):
    return x.ap if isinstance(x, V) else x


def _trks(xs):
    out = []
    for x in xs:
        if x is None:
            continue
        if isinstance(x, V):
            if x.trk is not None:
                out.append(x.trk)
        elif isinstance(x, Trk):
            out.append(x)
    return out


class Sched:
    SAME_ENG_RAW = True

    def __init__(self, nc):
        self.nc = nc
        self.eng = {"pe": nc.tensor, "act": nc.scalar, "dve": nc.vector,
                    "pool": nc.gpsimd, "sp": nc.sync}
        self.sem = {e: nc.alloc_semaphore("s_" + e) for e in ("pe", "act", "dve", "pool")}
        self.cnt = {e: 0 for e in self.sem}
        self.dq = {}
        for q, n in (("sp", 40), ("pool", 40), ("act", 8)):
            self.dq[q] = dict(sems=[nc.alloc_semaphore("d_%s%d" % (q, i)) for i in range(n)], k=0, n=n)
        self.waited = {e: {} for e in self.eng}
        self.nops = 0

    def _wait(self, stream, tok):
        key, sem, val, _ = tok
        w = self.waited[stream]
        if w.get(key, 0) >= val:
            return
        self.eng[stream].wait_ge(sem, val)
        w[key] = val

    def _deps(self, stream, eng, reads, writes):
        rt = _trks(reads)
        wt = _trks(writes)
        for t in rt:
            for tok in t.writers:
                if tok[3] == eng and eng is not None:
                    if eng == "pe" or not self.SAME_ENG_RAW:
                        continue
                self._wait(stream, tok)
            if t.excl:
                for tok in t.readers:
                    if tok[3] != eng:
                        self._wait(stream, tok)
        for t in wt:
            for tok in t.writers + t.readers:
                if tok[3] == eng and eng is not None:
                    continue
                self._wait(stream, tok)
        return rt, wt

    def _update(self, tok, rt, wt):
        for t in wt:
            t.writers = [tok]
            t.readers = []
        for t in rt:
            if t in wt:
                continue
            if tok[3] is not None:
                t.readers = [r for r in t.readers if r[3] != tok[3]]
            t.readers.append(tok)

    def op(self, eng, fn, reads=(), writes=()):
        rt, wt = self._deps(eng, eng, reads, writes)
        ins = fn()
        self.cnt[eng] += 1
        ins.then_inc(self.sem[eng], 1)
        tok = (eng, self.sem[eng], self.cnt[eng], eng)
        self._update(tok, rt, wt)
        self.nops += 1
        return tok

    def group(self, eng, fns, reads=(), writes=()):
        rt, wt = self._deps(eng, eng, reads, writes)
        ins = None
        for fn in fns:
            ins = fn()
        self.cnt[eng] += 1
        ins.then_inc(self.sem[eng], 1)
        tok = (eng, self.sem[eng], self.cnt[eng], eng)
        self._update(tok, rt, wt)
        self.nops += len(fns)
        return tok

    def dma(self, q, out, in_, reads=None, writes=None, slow=False):
        reads = [in_] if reads is None else reads
        writes = [out] if writes is None else writes
        rt, wt = self._deps(q, None, reads, writes)
        d = self.dq[q]
        k, n = d["k"], d["n"]
        sem = d["sems"][k % n]
        key = ("d", q, k % n)
        if k >= n:
            self._wait(q, (key, sem, 16 * (k // n), None))
        if slow:
            ins = self.eng[q].dma_start(out=_ap(out), in_=_ap(in_), allow_slow_non_contiguous=True)
        else:
            ins = self.eng[q].dma_start(out=_ap(out), in_=_ap(in_))
        ins.then_inc(sem, 16)
        d["k"] += 1
        tok = (key, sem, 16 * (k // n + 1), None)
        self._update(tok, rt, wt)
        self.nops += 1
        return tok

    def barrier(self):
        toks = []
        for e in self.sem:
            if self.cnt[e] > 0:
                toks.append((e, self.sem[e], self.cnt[e], e))
        for q, d in self.dq.items():
            k, n = d["k"], d["n"]
            for i in range(min(k, n)):
                uses = (k - i + n - 1) // n
                toks.append((("d", q, i), d["sems"][i], 16 * uses, None))
        for stream in self.eng:
            for tok in toks:
                if tok[3] == stream:
                    continue
                self._wait(stream, tok)

    def mm(self, out, pairs, extra_reads=()):
        nc = self.nc
        n = len(pairs)
        fns = []
        reads = list(extra_reads)
        for i, (l, r) in enumerate(pairs):
            reads += [l, r]
            fns.append(lambda l=l, r=r, i=i: nc.tensor.matmul(
                _ap(out), lhsT=_ap(l), rhs=_ap(r), start=(i == 0), stop=(i == n - 1)))
        return self.group("pe", fns, reads=reads, writes=[out])

    def transposes(self, items, ident):
        nc = self.nc
        fns, reads, writes = [], [ident], []
        for o, i in items:
            rows = _ap(i).shape[0]
            reads.append(i)
            writes.append(o)
            fns.append(lambda o=o, i=i, rows=rows: nc.tensor.transpose(
                out=_ap(o), in_=_ap(i), identity=_ap(ident)[0:rows, 0:rows]))
        return self.group("pe", fns, reads=reads, writes=writes)

    def act(self, out, in_, func, bias=None, scale=None, accum_out=None, eng="act"):
        nc = self.nc
        kw = {}
        reads = [in_]
        if bias is not None:
            kw["bias"] = _ap(bias)
            reads.append(bias)
        if scale is not None:
            kw["scale"] = _ap(scale)
            reads.append(scale)
        writes = [out]
        if accum_out is not None:
            kw["accum_out"] = _ap(accum_out)
            writes.append(accum_out)
        return self.op("act", lambda: nc.scalar.activation(out=_ap(out), in_=_ap(in_), func=func, **kw),
                       reads=reads, writes=writes)

    def _ve(self, eng):
        return self.nc.vector if eng == "dve" else self.nc.gpsimd

    def tt(self, out, in0, in1, op, eng="dve"):
        e = self._ve(eng)
        return self.op(eng, lambda: e.tensor_tensor(out=_ap(out), in0=_ap(in0), in1=_ap(in1), op=op),
                       reads=[in0, in1], writes=[out])

    def ts(self, out, in0, s1, op0, s2=None, op1=None, eng="dve"):
        e = self._ve(eng)
        kw = dict(scalar2=_ap(s2), op1=op1) if op1 is not None else dict(scalar2=None)
        return self.op(eng, lambda: e.tensor_scalar(out=_ap(out), in0=_ap(in0), scalar1=_ap(s1), op0=op0, **kw),
                       reads=[in0, s1, s2], writes=[out])

    def stt(self, out, in0, scalar, in1, op0, op1):
        nc = self.nc
        return self.op("dve", lambda: nc.vector.scalar_tensor_tensor(
            out=_ap(out), in0=_ap(in0), scalar=_ap(scalar), in1=_ap(in1), op0=op0, op1=op1),
            reads=[in0, scalar, in1], writes=[out])

    def copy(self, out, in_, eng="dve"):
        if eng == "act":
            return self.act(out, in_, AF.Copy)
        e = self._ve(eng)
        return self.op(eng, lambda: e.tensor_copy(out=_ap(out), in_=_ap(in_)), reads=[in_], writes=[out])

    def memset(self, out, val, eng="dve"):
        e = self._ve(eng)
        return self.op(eng, lambda: e.memset(_ap(out), val), reads=[], writes=[out])

    def recip(self, out, in_):
        nc = self.nc
        return self.op("dve", lambda: nc.vector.reciprocal(out=_ap(out), in_=_ap(in_)), reads=[in_], writes=[out])

    def scan(self, out, d0, d1, initial, op0, op1):
        nc = self.nc
        return self.op("dve", lambda: nc.vector.tensor_tensor_scan(
            out=_ap(out), data0=_ap(d0), data1=_ap(d1), initial=_ap(initial), op0=op0, op1=op1),
            reads=[d0, d1, initial], writes=[out])


class Builder:
    def __init__(self, seq, kinds):
        self.SEQ = seq
        self.NTOK = seq + NST
        self.kinds = list(kinds)
        self.nc = bass.Bass("TRN2", target_bir_lowering=False)
        self.S = Sched(self.nc)
        self.inputs = {}
        self.outputs = {}
        self._n = 0
        self.evac_i = 0
        self.debug = False
        self.dbg_out = {}

    def din(self, name, shape, dt=F32):
        ap = self.nc.dram_tensor(name, list(shape), dt, kind="ExternalInput").ap()
        self.inputs[name] = ap
        return ap

    def dout(self, name, shape, dt=F32):
        ap = self.nc.dram_tensor(name, list(shape), dt, kind="ExternalOutput").ap()
        self.outputs[name] = ap
        return V(ap, Trk(name))

    def dscr(self, name, shape, dt=F32):
        if getattr(self, "debug", False):
            ap = self.nc.dram_tensor(name, list(shape), dt, kind="ExternalOutput").ap()
            self.dbg_out[name] = ap
        else:
            ap = self.nc.dram_tensor(name, list(shape), dt, kind="Internal").ap()
        return V(ap, Trk(name))

    def sb(self, es, shape, dt=F32, name=None):
        self._n += 1
        name = "%s_%d" % (name or "t", self._n)
        t = es.enter_context(self.nc.sbuf_tensor(name, list(shape), dt))
        return V(t.ap() if hasattr(t, "ap") and callable(t.ap) else t, Trk(name))

    def dump(self, name, v, shape, dt):
        if not self.debug:
            return
        ap = self.nc.dram_tensor(name, list(shape), dt, kind="ExternalOutput").ap()
        self.dbg_out[name] = ap
        self.S.dma("sp", V(ap, Trk(name)), v)

    @contextmanager
    def phase(self):
        with ExitStack() as es:
            yield es
            self.S.barrier()

    def evac_eng(self):
        self.evac_i += 1
        return "act" if self.evac_i % 2 else "dve"

    def next_ps(self):
        self.ps_i = (self.ps_i + 1) % len(self.ps)
        return self.ps[self.ps_i]

    def build(self):
        nc, S = self.nc, self.S
        SEQ, NTOK = self.SEQ, self.NTOK
        with ExitStack() as es0:
            self.ps = []
            for i in range(8):
                t = es0.enter_context(nc.psum_tensor("ps%d" % i, [128, 512], F32))
                self.ps.append(V(t.ap() if callable(getattr(t, "ap", None)) else t, Trk("ps%d" % i, excl=True)))
            self.ps_i = 0
            ident_d = self.din("c_ident", [128, 128])
            self.ident = self.sb(es0, [128, 128], F32, "ident")
            S.dma("sp", self.ident, ident_d)
            self.identb = self.sb(es0, [128, 128], BF16, "identb")
            S.copy(self.identb, self.ident)

            xin = self.din("xin", [NTOK, D])
            norm_g = self.din("norm_g", [4, D])
            fin_g = self.din("final_norm_g", [D])
            y = self.dout("y", [NTOK, D])
            xres = [self.dscr("xres0", [NTOK, D]), self.dscr("xres1", [NTOK, D])]
            self.hgt = self.dscr("hgt", [E, NTOK], BF16)

            cur = V(xin, None)
            nk = len(self.kinds)
            for li, kind in enumerate(self.kinds):
                g_row = norm_g[kind, :]
                if kind == 3:
                    self.layer_pool(cur, g_row)
                elif kind == 2:
                    self.layer_attn(cur, g_row)
                elif kind == 1:
                    self.layer_s5(cur, g_row)
                else:
                    self.layer_mlstm(cur, g_row)
                S.barrier()
                dst = xres[li % 2]
                wout_name = {0: "a_w_out", 1: "b_w_out", 2: "c_w_out", 3: "d_w_out"}[kind]
                wout = self.din(wout_name, [E, D])
                self.phase_outproj(wout, cur, dst)
                S.barrier()
                cur = dst
            self.phase_norm(cur, fin_g, None, final_out=y)
            S.barrier()
        return nc

    def phase_norm(self, src, g_row, XNT, final_out=None):
        nc, S = self.nc, self.S
        NTOK = self.NTOK
        with self.phase() as es:
            gbc = self.sb(es, [128, D], F32, "gbc")
            S.dma("sp", gbc, g_row.partition_broadcast(128))
            xb = [self.sb(es, [128, D], F32, "xb%d" % i) for i in range(3)]
            xn = [self.sb(es, [128, D], F32, "xn%d" % i) for i in range(2)]
            junk = self.sb(es, [128, D], F32, "junk")
            st = [self.sb(es, [128, 4], F32, "st%d" % i) for i in range(3)]
            nt = (NTOK + 127) // 128
            for tt in range(nt):
                r0 = tt * 128
                rows = min(128, NTOK - r0)
                x = xb[tt % 3]
                s = st[tt % 3]
                o = xn[tt % 2]
                S.dma("sp", x[:rows], src[r0:r0 + rows, :])
                S.act(junk[:rows], x[:rows], AF.Square)
                S.op("dve", lambda rows=rows, s=s: nc.vector.tensor_reduce(out=s.ap[:rows, 0:1], in_=junk.ap[:rows], axis=AX.X, op=ALU.add),
                     reads=[junk], writes=[s])
                S.ts(s[:rows, 1:2], s[:rows, 0:1], 1.0 / D, ALU.mult, RMS_EPS, ALU.add)
                S.act(s[:rows, 2:3], s[:rows, 1:2], AF.Sqrt)
                S.recip(s[:rows, 3:4], s[:rows, 2:3])
                S.stt(o[:rows], x[:rows], s[:rows, 3:4], gbc[:rows], ALU.mult, ALU.mult)
                if final_out is not None:
                    S.dma("sp", final_out[r0:r0 + rows, :], o[:rows])
                    continue
                t512, off = divmod(r0, 512)
                for half in range(2):
                    ps = self.next_ps()
                    items = []
                    for j in range(4):
                        dc = half * 4 + j
                        items.append((ps[:, j * 128:j * 128 + rows], o[:rows, dc * 128:(dc + 1) * 128]))
                    S.transposes(items, self.ident)
                    dstv = XNT[t512][:, half * 4:half * 4 + 4, off:off + rows]
                    srcv = ps.re("p (j t) -> p j t", j=4)[:, :, 0:rows]
                    S.copy(dstv, srcv, eng=self.evac_eng())
            if final_out is None:
                self.dump("dbg_xn1", xn[1], [128, D], F32)
                self.dump("dbg_st0", st[0], [128, 4], F32)
                self.dump("dbg_gbc", gbc, [128, D], F32)

    def inproj_fm(self, w_dram, XNT, segs, es):
        nc, S = self.nc, self.S
        NTOK = self.NTOK
        wb = [self.sb(es, [128, 8, 512], BF16, "wb%d" % i) for i in range(2)]
        stg = {}
        wi = 0
        si = 0
        wv = w_dram.rearrange("(kc p) n -> p kc n", p=128)
        for (c0, ncols, dst, dt) in segs:
            if dt not in stg:
                stg[dt] = [self.sb(es, [128, 512], dt, "stg%d" % i) for i in range(3)]
            for cb in range(0, ncols, 512):
                n = min(512, ncols - cb)
                w = wb[wi % 2]
                wi += 1
                S.dma("pool", w[:, :, 0:n], wv[:, :, c0 + cb:c0 + cb + n])
                for jb in range(0, n, 128):
                    m = min(128, n - jb)
                    for tt in range(len(XNT)):
                        ntk = min(512, NTOK - tt * 512)
                        ps = self.next_ps()
                        S.mm(ps[:m, :ntk], [(w[:, kc, jb:jb + m], XNT[tt][:, kc, :ntk]) for kc in range(8)])
                        sg = stg[dt][si % 3]
                        si += 1
                        S.copy(sg[:m, :ntk], ps[:m, :ntk], eng=self.evac_eng())
                        S.dma("sp", dst[cb + jb:cb + jb + m, tt * 512:tt * 512 + ntk], sg[:m, :ntk])

    def phase_outproj(self, wout, src, dst):
        nc, S = self.nc, self.S
        NTOK = self.NTOK
        with self.phase() as es:
            wo = self.sb(es, [128, 16, D], BF16, "wo")
            wv = wout.rearrange("(kc p) n -> p kc n", p=128)
            for i in range(4):
                S.dma("pool", wo[:, 4 * i:4 * i + 4, :], wv[:, 4 * i:4 * i + 4, :])
            hgb = [self.sb(es, [128, 16, 512], BF16, "hg%d" % i) for i in range(2)]
            xb = [self.sb(es, [128, D], F32, "xo%d" % i) for i in range(3)]
            hv = self.hgt.re("(kc p) t -> p kc t", p=128)
            nt512 = (NTOK + 511) // 512
            xi = 0
            for tt in range(nt512):
                ntk = min(512, NTOK - tt * 512)
                hg = hgb[tt % 2]
                S.dma("sp", hg[:, :, :ntk], hv[:, :, tt * 512:tt * 512 + ntk])
                for sub in range(0, ntk, 128):
                    rows = min(128, ntk - sub)
                    r0 = tt * 512 + sub
                    x = xb[xi % 3]
                    xi += 1
                    S.dma("sp", x[:rows], src[r0:r0 + rows, :])
                    for half in range(2):
                        ps = self.next_ps()
                        S.mm(ps[:rows, :], [(hg[:, kc, sub:sub + rows], wo[:, kc, half * 512:(half + 1) * 512])
                                            for kc in range(16)])
                        S.tt(x[:rows, half * 512:(half + 1) * 512], x[:rows, half * 512:(half + 1) * 512],
                             ps[:rows, :], ALU.add)
                    S.dma("sp", dst[r0:r0 + rows, :], x[:rows])


    def inproj_tm(self, w_dram, XNT, c0, ncols, dst, es):
        nc, S = self.nc, self.S
        NTOK = self.NTOK
        wb = [self.sb(es, [128, 8, 512], BF16, "wtm%d" % i) for i in range(2)]
        stg = [self.sb(es, [128, 512], F32, "stm%d" % i) for i in range(3)]
        wv = w_dram.rearrange("(kc p) n -> p kc n", p=128)
        si = 0
        for ci, cb in enumerate(range(0, ncols, 512)):
            n = min(512, ncols - cb)
            w = wb[ci % 2]
            S.dma("pool", w[:, :, 0:n], wv[:, :, c0 + cb:c0 + cb + n])
            for tt in range(len(XNT)):
                ntk = min(512, NTOK - tt * 512)
                for sub in range(0, ntk, 128):
                    rows = min(128, ntk - sub)
                    r0 = tt * 512 + sub
                    ps = self.next_ps()
                    S.mm(ps[:rows, :n], [(XNT[tt][:, kc, sub:sub + rows], w[:, kc, :n]) for kc in range(8)])
                    sg = stg[si % 3]
                    si += 1
                    S.copy(sg[:rows, :n], ps[:rows, :n], eng=self.evac_eng())
                    S.dma("sp", dst[r0:r0 + rows, cb:cb + n], sg[:rows, :n])


    def layer_mlstm(self, cur, g_row):
        nc, S = self.nc, self.S
        SEQ, NTOK = self.SEQ, self.NTOK
        H = 4
        w_in = self.din("a_w_in", [D, 6152])
        b_gate = self.din("a_b_gate", [8])
        conv_w = self.din("a_conv_w", [4, E])
        conv_b = self.din("a_conv_b", [E])
        wq_d = self.din("a_w_q", [H, 512, 512])
        wk_d = self.din("a_w_k", [H, 512, 512])
        wv_d = self.din("a_w_v", [H, 512, 512])
        ng_d = self.din("a_norm_g", [E])
        sk_d = self.din("a_skip", [E])
        c_in = self.din("state_mlstm_c", [NSMP, H, 512, 512])
        n_in = self.din("state_mlstm_n", [NSMP, H, 512])
        m_in = self.din("state_mlstm_m", [NSMP, H])
        cv_in = self.din("state_mlstm_conv", [NSMP, 3, E])
        sel_d = self.din("c_sel4", [4, 4 * 128])
        i4_d = self.din("c_ident4", [4, 4])
        mneg_d = self.din("c_maskneg", [128, 128])
        c_p = self.dout("mlstm_c_p", [H, 512, 512])
        c_s = self.dout("mlstm_c_s", [NSMP, H, 512, 512])
        n_p = self.dout("mlstm_n_p", [H, 512])
        n_s = self.dout("mlstm_n_s", [NSMP, H, 512])
        m_p = self.dout("mlstm_m_p", [H, 1])
        m_s = self.dout("mlstm_m_s", [H, NSMP])
        cv_p = self.dout("mlstm_conv_p", [3, E])
        cv_s = self.dout("mlstm_conv_s", [NSMP, 3, E])
        xmT = self.dscr("ml_xmT", [E, NTOK], F32)
        zT = self.dscr("ml_zT", [E, NTOK], BF16)
        xcT = self.dscr("ml_xcT", [E, NTOK], BF16)
        otok = self.dscr("ml_otok", [NTOK, E], F32)
        giT = self.dscr("ml_gi", [4, NTOK], F32)
        gfT = self.dscr("ml_gf", [4, NTOK], F32)
        with self.phase() as es:
            XNT = self.alloc_xnt(es)
            self.phase_norm(cur, g_row, XNT)
            self.inproj_fm(w_in, XNT, [(0, E, xmT, F32), (E, E, zT, BF16), (6144, 4, giT, F32), (6148, 4, gfT, F32)], es)
            self.inproj_tm(w_in, XNT, 4096, E, otok, es)

        with self.phase() as es:
            cw = self.sb(es, [128, 16, 4], F32, "cw")
            cb = self.sb(es, [128, 16], F32, "cb")
            for i in range(4):
                S.dma("sp", cw[:, :, i], V(conv_w[i, :].rearrange("(c p) -> p c", p=128), None), slow=True)
            S.dma("sp", cb, V(conv_b.rearrange("(c p) -> p c", p=128), None), slow=True)
            ext = [self.sb(es, [128, 3 + SEQ], F32, "cext%d" % i) for i in range(2)]
            acc = [self.sb(es, [128, SEQ], F32, "cacc%d" % i) for i in range(2)]
            xo = [self.sb(es, [128, NTOK], BF16, "cxo%d" % i) for i in range(2)]
            exs = [self.sb(es, [128, NSMP, 3 + LS], F32, "cexs%d" % i) for i in range(2)]
            acs = [self.sb(es, [128, NSMP, LS], F32, "cacs%d" % i) for i in range(2)]
            for cc in range(16):
                ex, ac, o, xs, as_ = ext[cc % 2], acc[cc % 2], xo[cc % 2], exs[cc % 2], acs[cc % 2]
                rows = slice(cc * 128, (cc + 1) * 128)
                S.memset(ex[:, 0:3], 0.0, eng="pool")
                S.dma("sp", ex[:, 3:3 + SEQ], xmT[rows, 0:SEQ])
                for s_ in range(NSMP):
                    S.dma("sp", xs[:, s_, 0:3], V(cv_in[s_, :, rows].rearrange("r p -> p r"), None), slow=True)
                S.dma("sp", xs[:, :, 3:3 + LS], xmT[rows, SEQ:SEQ + NST].re("p (s t) -> p s t", t=LS))
                S.ts(ac, ex[:, 0:SEQ], cw[:, cc, 0:1], ALU.mult)
                S.ts(as_, xs[:, :, 0:LS], cw[:, cc, 0:1], ALU.mult)
                for i in range(1, 4):
                    S.stt(ac, ex[:, i:i + SEQ], cw[:, cc, i:i + 1], ac, ALU.mult, ALU.add)
                    S.stt(as_, xs[:, :, i:i + LS], cw[:, cc, i:i + 1], as_, ALU.mult, ALU.add)
                S.act(o[:, 0:SEQ], ac, AF.Silu, bias=cb[:, cc:cc + 1])
                S.act(o[:, SEQ:SEQ + NST].re("p (s t) -> p s t", t=LS), as_, AF.Silu, bias=cb[:, cc:cc + 1])
                S.dma("sp", xcT[rows, :], o)
                S.dma("sp", cv_p[:, rows].re("r p -> p r"), ex[:, SEQ:SEQ + 3], slow=True)
                for s_ in range(NSMP):
                    S.dma("sp", cv_s[s_, :, rows].re("r p -> p r"), xs[:, s_, LS:LS + 3], slow=True)

        NC_ = SEQ // 128
        U = NC_ + NSMP
        units = [(c * 128, 128) for c in range(NC_)] + [(SEQ + s * LS, LS) for s in range(NSMP)]
        with self.phase() as es:
            sel = self.sb(es, [4, 512], F32, "sel")
            S.dma("sp", sel, sel_d)
            i4 = self.sb(es, [4, 4], F32, "i4")
            S.dma("sp", i4, i4_d)
            mneg = self.sb(es, [128, 128], F32, "mneg")
            S.dma("sp", mneg, mneg_d)
            ones4 = self.sb(es, [4, 128], F32, "ones4")
            S.memset(ones4, 1.0)
            onesb = self.sb(es, [128, 1], BF16, "onesb1")
            S.memset(onesb, 1.0)
            negR = self.sb(es, [4, NTOK], F32, "negR")
            cols = self.sb(es, [128, U, 3, 4], F32, "cols")
            a_c = self.sb(es, [128, U, 4], F32, "a_c")
            ws_c = self.sb(es, [128, U, 4], F32, "ws_c")
            dc_c = self.sb(es, [128, U, 4], F32, "dc_c")
            em_c = self.sb(es, [128, U, 4], F32, "em_c")
            with self.phase() as es2:
                bi_ = self.sb(es2, [4, 1], F32, "bi")
                nbf = self.sb(es2, [4, 1], F32, "nbf")
                S.dma("sp", bi_, V(b_gate[0:4].rearrange("(p o) -> p o", o=1), None), slow=True)
                S.dma("sp", nbf, V(b_gate[4:8].rearrange("(p o) -> p o", o=1), None), slow=True)
                S.ts(nbf, nbf, -1.0, ALU.mult)
                m0 = self.sb(es2, [4, NSMP], F32, "m0")
                S.dma("sp", m0, V(m_in.rearrange("s h -> h s"), None), slow=True)
                ig = self.sb(es2, [4, NTOK], F32, "ig")
                lf = self.sb(es2, [4, NTOK], F32, "lf")
                Fg = self.sb(es2, [4, NTOK], F32, "Fg")
                gg = self.sb(es2, [4, NTOK], F32, "gg")
                R = self.sb(es2, [4, NTOK], F32, "R")
                nmt = self.sb(es2, [4, NTOK], F32, "nmt")
                onesr = self.sb(es2, [4, NTOK], F32, "onesr")
                S.memset(onesr, 1.0)
                S.dma("sp", ig, giT)
                S.dma("sp", lf, gfT)
                S.act(ig, ig, AF.Identity, bias=bi_)
                S.act(lf, lf, AF.Exp, bias=nbf, scale=-1.0)
                S.act(lf, lf, AF.Ln, bias=1.0)
                S.ts(lf, lf, -1.0, ALU.mult)
                segs = [(0, SEQ, None)] + [(SEQ + s * LS, LS, s) for s in range(NSMP)]
                for (t0, n, s_) in segs:
                    sl = slice(t0, t0 + n)
                    S.scan(Fg[:, sl], onesr[:, sl], lf[:, sl], 0.0, ALU.mult, ALU.add)
                S.tt(gg, ig, Fg, ALU.subtract)
                for (t0, n, s_) in segs:
                    sl = slice(t0, t0 + n)
                    init = 0.0 if s_ is None else m0[:, s_:s_ + 1]
                    S.scan(R[:, sl], gg[:, sl], gg[:, sl], init, ALU.max, ALU.max)
                S.ts(negR, R, -1.0, ALU.mult)
                S.tt(nmt, negR, Fg, ALU.subtract)
                mo = self.sb(es2, [4, 1 + NSMP], F32, "mo")
                S.ts(mo[:, 0:1], nmt[:, SEQ - 1:SEQ], -1.0, ALU.mult)
                S.ts(mo[:, 1:1 + NSMP], nmt[:, SEQ + LS - 1:SEQ + NST:LS], -1.0, ALU.mult)
                S.dma("sp", m_p, mo[:, 0:1], slow=True)
                S.dma("sp", m_s, mo[:, 1:1 + NSMP], slow=True)
                Rs = self.sb(es2, [4, U], F32, "Rs")
                Re = self.sb(es2, [4, U], F32, "Re")
                S.copy(Re[:, 0:NC_], R[:, 127:SEQ:128])
                S.copy(Re[:, NC_:U], R[:, SEQ + LS - 1:SEQ + NST:LS])
                S.memset(Rs[:, 0:1], 0.0)
                if NC_ > 1:
                    S.copy(Rs[:, 1:NC_], Re[:, 0:NC_ - 1])
                S.copy(Rs[:, NC_:U], m0)
                bcs = self.sb(es2, [128, U, 4], F32, "bcs")
                bce = self.sb(es2, [128, U, 4], F32, "bce")
                tmp4 = self.sb(es2, [4, U, 4], F32, "tmp4")
                for rows_, dst in ((Rs, bcs), (Re, bce)):
                    S.tt(tmp4, V(rows_.ap.unsqueeze(2).to_broadcast([4, U, 4]), rows_.trk),
                         V(i4.ap.unsqueeze(1).to_broadcast([4, U, 4]), i4.trk), ALU.mult)
                    ps = self.next_ps()
                    S.mm(ps[:, 0:U * 4], [(ones4, tmp4.re("p u h -> p (u h)"))])
                    S.copy(dst.re("p u h -> p (u h)"), ps[:, 0:U * 4])
                S.memset(cols, 0.0)
                for u, (t0, T) in enumerate(units):
                    ps = self.next_ps()
                    S.transposes([(ps[:T, q * 4:(q + 1) * 4], src[:, t0:t0 + T]) for q, src in enumerate((gg, negR, nmt))],
                                 self.ident)
                    S.copy(cols[:T, u].re("p q h -> p (q h)"), ps[:T, 0:12], eng=self.evac_eng())
                S.tt(a_c, cols[:, :, 1, :], bcs, ALU.add)
                S.act(a_c, a_c, AF.Exp)
                S.tt(ws_c, cols[:, :, 0, :], bce, ALU.subtract)
                S.act(ws_c, ws_c, AF.Exp)
                S.tt(dc_c, bcs, bce, ALU.subtract)
                S.act(dc_c, dc_c, AF.Exp)
                S.act(em_c, cols[:, :, 2, :], AF.Exp)
            ngc = self.sb(es, [128, 16], F32, "ngc")
            skc = self.sb(es, [128, 16], F32, "skc")
            S.dma("sp", ngc, V(ng_d.rearrange("(c p) -> p c", p=128), None), slow=True)
            S.dma("sp", skc, V(sk_d.rearrange("(c p) -> p c", p=128), None), slow=True)

            wq = self.sb(es, [128, 4, 512], BF16, "wq")
            wk = self.sb(es, [128, 4, 512], BF16, "wk")
            wv = self.sb(es, [128, 4, 512], BF16, "wv")
            xc_h = self.sb(es, [128, 4, NTOK], BF16, "xc_h")
            xm_h = self.sb(es, [128, 4, NTOK], BF16, "xm_h")
            z_h = self.sb(es, [128, 4, NTOK], BF16, "z_h")
            sxu = [self.sb(es, [128, 4, 128], F32, "sxu%d" % i) for i in range(2)]
            hgu = [self.sb(es, [128, 4, 128], BF16, "hgu%d" % i) for i in range(2)]
            C = self.sb(es, [128, 4, 512], F32, "C")
            Cb = self.sb(es, [128, 4, 512], BF16, "Cb")
            nv = self.sb(es, [128, 4], F32, "nv")
            nb = self.sb(es, [128, 4], BF16, "nb")
            ot = [self.sb(es, [128, 512], F32, "ot%d" % i) for i in range(2)]
            qT = [self.sb(es, [128, 4, 128], BF16, "qTs%d" % i) for i in range(2)]
            kT = [self.sb(es, [128, 4, 128], BF16, "kTs%d" % i) for i in range(2)]
            kw = [self.sb(es, [128, 512], BF16, "kw%d" % i) for i in range(2)]
            vt = [self.sb(es, [128, 512], BF16, "vt%d" % i) for i in range(2)]
            WT = [self.sb(es, [128, 128], F32, "WT%d" % i) for i in range(2)]
            ST = [self.sb(es, [128, 128], BF16, "ST%d" % i) for i in range(2)]
            sm = [self.sb(es, [128, 16], F32, "sm%d" % i) for i in range(2)]
            P2s = [self.sb(es, [128, 512], F32, "P2s%d" % i) for i in range(2)]
            hc = [self.sb(es, [128, 512], F32, "hc%d" % i) for i in range(2)]
            hn = [self.sb(es, [128, 512], F32, "hn%d" % i) for i in range(2)]
            hh = [self.sb(es, [128, 4, 128], F32, "hh%d" % i) for i in range(2)]
            isq = 1.0 / math.sqrt(512.0)
            ui = 0
            for h in range(H):
                for wt_, wd in ((wq, wq_d), (wk, wk_d), (wv, wv_d)):
                    S.dma("pool", wt_, V(wd[h].rearrange("(kc p) n -> p kc n", p=128), None))
                hr = slice(h * 512, (h + 1) * 512)
                S.dma("sp", xc_h, xcT[hr, :].re("(kc p) t -> p kc t", p=128))
                S.dma("pool", xm_h, xmT[hr, :].re("(kc p) t -> p kc t", p=128))
                S.dma("sp", z_h, zT[hr, :].re("(kc p) t -> p kc t", p=128))
                S.act(z_h, z_h, AF.Silu)
                for u, (t0, T) in enumerate(units):
                    smp = u - NC_
                    tk = slice(t0, t0 + T)
                    if u == 0:
                        S.memset(C, 0.0, eng="pool")
                        S.memset(Cb, 0.0, eng="pool")
                        S.memset(nv, 0.0, eng="pool")
                        S.memset(nb, 0.0, eng="pool")
                    elif smp >= 0:
                        if smp == 0:
                            S.dma("sp", c_p[h].re("(db p) e -> p db e", p=128), C)
                            S.dma("sp", n_p[h].re("(db p) -> p db", p=128), nv, slow=True)
                        S.dma("sp", C, V(c_in[smp, h].rearrange("(db p) e -> p db e", p=128), None))
                        S.dma("sp", nv, V(n_in[smp, h].rearrange("(db p) -> p db", p=128), None), slow=True)
                        S.copy(Cb, C, eng="act")
                        S.copy(nb, nv)
                    b2 = ui % 2
                    ui += 1
                    o_t, q_s, k_s, kw_s, v_s = ot[b2], qT[b2], kT[b2], kw[b2], vt[b2]
                    S.dma("sp", o_t[:T], otok[t0:t0 + T, hr])
                    ps = self.next_ps()
                    for eb in range(4):
                        S.mm(ps[:, eb * 128:eb * 128 + T],
                             [(wq[:, kc, eb * 128:(eb + 1) * 128], xc_h[:, kc, tk]) for kc in range(4)])
                    S.copy(q_s[:, :, :T], ps.re("p (e t) -> p e t", e=4)[:, :, :T], eng="act")
                    ps = self.next_ps()
                    for eb in range(4):
                        S.mm(ps[:, eb * 128:eb * 128 + T],
                             [(wk[:, kc, eb * 128:(eb + 1) * 128], xc_h[:, kc, tk]) for kc in range(4)])
                    S.ts(k_s[:, :, :T], ps.re("p (e t) -> p e t", e=4)[:, :, :T], isq, ALU.mult)
                    ps = self.next_ps()
                    S.mm(ps[:T, :], [(xc_h[:, kc, tk], wk[:, kc, :]) for kc in range(4)])
                    S.ts(kw_s[:T], ps[:T, :], ws_c[:T, u, h:h + 1], ALU.mult, isq, ALU.mult)
                    ps = self.next_ps()
                    S.mm(ps[:T, :], [(xm_h[:, kc, tk], wv[:, kc, :]) for kc in range(4)])
                    S.copy(v_s[:T], ps[:T, :], eng="act")
                    psB = self.next_ps()
                    S.mm(psB[:T, :T], [(sel[:, h * 128:h * 128 + T], negR[:, tk]),
                                       (self.ident[:T, :T], mneg[:T, :T])])
                    S.act(WT[b2][:T, :T], psB[:T, :T], AF.Exp, bias=cols[:T, u, 0, h:h + 1])
                    psA = self.next_ps()
                    S.mm(psA[:T, :T], [(k_s[:, eb, :T], q_s[:, eb, :T]) for eb in range(4)])
                    S.tt(ST[b2][:T, :T], psA[:T, :T], WT[b2][:T, :T], ALU.mult)
                    ps1 = self.next_ps()
                    S.mm(ps1[:T, :], [(q_s[:, db, :T], Cb[:, db, :]) for db in range(4)])
                    ps2 = self.next_ps()
                    S.mm(ps2[:T, :], [(ST[b2][:T, :T], v_s[:T, :])])
                    psD = self.next_ps()
                    S.mm(psD[:T, 0:1], [(q_s[:, db, :T], nb[:, db:db + 1]) for db in range(4)])
                    S.mm(psD[:T, 1:2], [(ST[b2][:T, :T], onesb[:T, 0:1])])
                    m_ = sm[b2]
                    S.copy(m_[:T, 0:2], psD[:T, 0:2])
                    S.stt(m_[:T, 2:3], m_[:T, 0:1], a_c[:T, u, h:h + 1], m_[:T, 1:2], ALU.mult, ALU.add)
                    S.stt(m_[:T, 3:4], m_[:T, 2:3], -1.0, m_[:T, 2:3], ALU.mult, ALU.max)
                    S.tt(m_[:T, 4:5], m_[:T, 3:4], em_c[:T, u, h:h + 1], ALU.max)
                    S.recip(m_[:T, 5:6], m_[:T, 4:5])
                    S.tt(m_[:T, 6:7], m_[:T, 5:6], a_c[:T, u, h:h + 1], ALU.mult)
                    S.act(P2s[b2][:T], ps2[:T, :], AF.Identity, scale=m_[:T, 5:6])
                    S.stt(hc[b2][:T], ps1[:T, :], m_[:T, 6:7], P2s[b2][:T], ALU.mult, ALU.add)
                    S.act(o_t[:T], o_t[:T], AF.Sigmoid)
                    S.tt(hc[b2][:T], hc[b2][:T], o_t[:T], ALU.mult)
                    S.op("dve", lambda b2=b2, T=T, m_=m_: nc.vector.bn_stats(out=m_.ap[:T, 8:14], in_=hc[b2].ap[:T]),
                         reads=[hc[b2]], writes=[m_])
                    S.op("dve", lambda T=T, m_=m_: nc.vector.bn_aggr(out=m_.ap[:T, 14:16], in_=m_.ap[:T, 8:14]),
                         reads=[m_], writes=[m_])
                    S.ts(m_[:T, 7:8], m_[:T, 15:16], 1e-6, ALU.add)
                    S.act(m_[:T, 7:8], m_[:T, 7:8], AF.Sqrt)
                    S.recip(m_[:T, 7:8], m_[:T, 7:8])
                    S.stt(m_[:T, 8:9], m_[:T, 14:15], -1.0, m_[:T, 7:8], ALU.mult, ALU.mult)
                    S.act(hn[b2][:T], hc[b2][:T], AF.Identity, scale=m_[:T, 7:8], bias=m_[:T, 8:9])
                    psT = self.next_ps()
                    S.transposes([(psT[:, eb * 128:eb * 128 + T], hn[b2][:T, eb * 128:(eb + 1) * 128]) for eb in range(4)],
                                 self.ident)
                    for eb in range(4):
                        S.ts(sxu[b2][:, eb, :T], xc_h[:, eb, tk], skc[:, h * 4 + eb:h * 4 + eb + 1], ALU.mult)
                        S.stt(hh[b2][:, eb, :T], psT[:, eb * 128:eb * 128 + T], ngc[:, h * 4 + eb:h * 4 + eb + 1],
                              sxu[b2][:, eb, :T], ALU.mult, ALU.add)
                    S.tt(hgu[b2][:, :, :T], hh[b2][:, :, :T], z_h[:, :, tk], ALU.mult, eng="pool")
                    S.dma("sp", self.hgt[hr, tk].re("(kc p) t -> p kc t", p=128), hgu[b2][:, :, :T])
                    for db in range(4):
                        psC = self.next_ps()
                        S.mm(psC, [(kw_s[:T, db * 128:(db + 1) * 128], v_s[:T, :])])
                        S.stt(C[:, db, :], C[:, db, :], dc_c[:, u, h:h + 1], psC, ALU.mult, ALU.add)
                    psN = self.next_ps()
                    for db in range(4):
                        S.mm(psN[:, db:db + 1], [(kw_s[:T, db * 128:(db + 1) * 128], onesb[:T, 0:1])])
                    S.stt(nv, nv, dc_c[:, u, h:h + 1], psN[:, 0:4], ALU.mult, ALU.add)
                    if smp < 0:
                        S.copy(Cb, C, eng="act")
                        S.copy(nb, nv)
                    else:
                        S.dma("sp", c_s[smp, h].re("(db p) e -> p db e", p=128), C)
                        S.dma("sp", n_s[smp, h].re("(db p) -> p db", p=128), nv, slow=True)


    def s5_sincos(self, es, ang, sin_o, cos_o, n):
        S = self.S
        I32 = mybir.dt.int32
        ki = self.sb(es, [128, n], I32, "sc_ki")
        kf = self.sb(es, [128, n], F32, "sc_kf")
        r = self.sb(es, [128, n], F32, "sc_r")
        m = self.sb(es, [128, n], F32, "sc_m")
        S.copy(ki, ang)
        S.copy(kf, ki)
        S.tt(r, ang, kf, ALU.subtract)
        for shift, dst in ((0.0, sin_o), (0.25, cos_o)):
            if shift:
                S.ts(r, r, shift, ALU.add)
            S.ts(m, r, 0.5, ALU.is_gt)
            S.tt(r, r, m, ALU.subtract)
            S.ts(m, r, -0.5, ALU.is_lt)
            S.tt(r, r, m, ALU.add)
            S.act(dst, r, AF.Sin, scale=6.28318)

    def layer_s5(self, cur, g_row):
        nc, S = self.nc, self.S
        SEQ, NTOK = self.SEQ, self.NTOK
        w_in = self.din("b_w_in", [D, 2 * E])
        lam_re = self.din("b_lam_re", [128, 64])
        lam_im = self.din("b_lam_im", [128, 64])
        log_dt = self.din("b_log_dt", [128])
        B_re = self.din("b_B_re", [128, 64, 16])
        B_im = self.din("b_B_im", [128, 64, 16])
        C_re = self.din("b_C_re", [128, 16, 64])
        C_im = self.din("b_C_im", [128, 16, 64])
        d_d = self.din("b_d", [E])
        w_glu = self.din("b_w_glu", [E, E])
        st_re = self.din("state_s5_re", [NSMP, 128, 64])
        st_im = self.din("state_s5_im", [NSMP, 128, 64])
        par_d = self.din("c_par", [128, 2])
        selg_d = self.din("c_selg", [128, 64])
        iota_d = self.din("c_iota64", [64])
        o_re_p = self.dout("s5_re_p", [128, 64])
        o_im_p = self.dout("s5_im_p", [128, 64])
        o_re_s = self.dout("s5_re_s", [NSMP, 128, 64])
        o_im_s = self.dout("s5_im_s", [NSMP, 128, 64])
        uT = self.dscr("s5_uT", [E, NTOK], F32)
        zT = self.dscr("s5_zT", [E, NTOK], BF16)
        gT = self.dscr("s5_gT", [E, NTOK], BF16)
        with self.phase() as es:
            XNT = self.alloc_xnt(es)
            self.phase_norm(cur, g_row, XNT)
            self.inproj_fm(w_in, XNT, [(0, E, uT, F32), (E, E, zT, BF16)], es)

        def qi(ap2d):
            return ap2d.rearrange("g p -> (g p)").rearrange("(i q) -> q i", q=128)

        HALF = min(1024, SEQ)
        NA = HALF // 64
        with self.phase() as es:
            sm = lambda nm: self.sb(es, [128, 64], F32, nm)
            lr, li, dtl, dt, rho, th, are, aim, cr, ci, t1, t2, t3 = [sm("s5p%d" % k) for k in range(13)]
            e1r, e1i = sm("e1r"), sm("e1i")
            S.dma("sp", lr, V(qi(lam_re), None), slow=True)
            S.dma("sp", li, V(qi(lam_im), None), slow=True)
            ld2 = log_dt.rearrange("(i g2) -> g2 i", g2=2)
            for g2 in range(2):
                S.dma("sp", dtl[g2 * 64:(g2 + 1) * 64, :], V(ld2[g2].partition_broadcast(64), None), slow=True)
            EBr = self.sb(es, [128, 64, 64], F32, "EBr")
            EBi = self.sb(es, [128, 64, 64], F32, "EBi")
            EAr = self.sb(es, [128, 64, NA], F32, "EAr")
            EAi = self.sb(es, [128, 64, NA], F32, "EAi")
            bbr = self.sb(es, [128, 64, 16], F32, "bbr")
            bbi = self.sb(es, [128, 64, 16], F32, "bbi")
            CQr = self.sb(es, [128, 64, 16], F32, "CQr")
            CQi = self.sb(es, [128, 64, 16], F32, "CQi")
            h0r = self.sb(es, [128, NSMP, 64], F32, "h0r")
            h0i = self.sb(es, [128, NSMP, 64], F32, "h0i")
            for s_ in range(NSMP):
                S.dma("sp", h0r[:, s_, :], V(qi(st_re[s_]), None), slow=True)
                S.dma("sp", h0i[:, s_, :], V(qi(st_im[s_]), None), slow=True)
            dcol = self.sb(es, [128, 16], F32, "dcol")
            S.dma("sp", dcol, V(d_d.rearrange("(c p) -> p c", p=128), None), slow=True)
            with self.phase() as es2:
                S.act(dt, dtl, AF.Exp)
                S.ts(lr, lr, -1e-4, ALU.min)
                S.tt(t1, dt, lr, ALU.mult)
                S.act(rho, t1, AF.Exp)
                S.tt(th, dt, li, ALU.mult)
                S.ts(th, th, 1.0 / (2.0 * math.pi), ALU.mult)
                self.s5_sincos(es2, th, e1i, e1r, 64)
                S.tt(are, rho, e1r, ALU.mult)
                S.tt(aim, rho, e1i, ALU.mult)
                S.ts(t1, are, -1.0, ALU.add)
                S.tt(t2, lr, lr, ALU.mult)
                S.tt(t3, li, li, ALU.mult)
                S.tt(t2, t2, t3, ALU.add)
                S.recip(t2, t2)
                S.tt(cr, t1, lr, ALU.mult)
                S.tt(t3, aim, li, ALU.mult)
                S.tt(cr, cr, t3, ALU.add)
                S.tt(cr, cr, t2, ALU.mult)
                S.tt(ci, aim, lr, ALU.mult)
                S.tt(t3, t1, li, ALU.mult)
                S.tt(ci, ci, t3, ALU.subtract)
                S.tt(ci, ci, t2, ALU.mult)
                iot = self.sb(es2, [128, 64], F32, "iot")
                S.dma("sp", iot, V(iota_d.partition_broadcast(128), None))
                ang = self.sb(es2, [128, 64, 64], F32, "ang")
                S.tt(ang, V(th.ap.unsqueeze(2).to_broadcast([128, 64, 64]), th.trk),
                     V(iot.ap.unsqueeze(1).to_broadcast([128, 64, 64]), iot.trk), ALU.mult)
                self.s5_sincos(es2, ang.re("p i b -> p (i b)"), EBi.re("p i b -> p (i b)"),
                               EBr.re("p i b -> p (i b)"), 4096)
                I32 = mybir.dt.int32
                k64 = self.sb(es2, [128, 64], I32, "k64")
                S.ts(t1, th, 64.0, ALU.mult)
                S.copy(k64, t1)
                S.copy(t3, k64)
                S.tt(t1, t1, t3, ALU.subtract)
                angA = self.sb(es2, [128, 64, NA], F32, "angA")
                S.tt(angA, V(t1.ap.unsqueeze(2).to_broadcast([128, 64, NA]), t1.trk),
                     V(iot.ap[:, 0:NA].unsqueeze(1).to_broadcast([128, 64, NA]), iot.trk), ALU.mult)
                self.s5_sincos(es2, angA.re("p i a -> p (i a)"), EAi.re("p i a -> p (i a)"),
                               EAr.re("p i a -> p (i a)"), 64 * NA)
                Br = self.sb(es2, [128, 64, 16], F32, "Br")
                Bi = self.sb(es2, [128, 64, 16], F32, "Bi")
                bq = lambda a: a.rearrange("g p c -> (g p) c").rearrange("(i q) c -> q i c", q=128)
                S.dma("sp", Br, V(bq(B_re), None))
                S.dma("sp", Bi, V(bq(B_im), None))
                crb = V(cr.ap.unsqueeze(2).to_broadcast([128, 64, 16]), cr.trk)
                cib = V(ci.ap.unsqueeze(2).to_broadcast([128, 64, 16]), ci.trk)
                tb = self.sb(es2, [128, 64, 16], F32, "tb")
                S.tt(bbr, Br, crb, ALU.mult)
                S.tt(tb, Bi, cib, ALU.mult)
                S.tt(bbr, bbr, tb, ALU.subtract)
                S.tt(bbi, Bi, crb, ALU.mult)
                S.tt(tb, Br, cib, ALU.mult)
                S.tt(bbi, bbi, tb, ALU.add)
                par = self.sb(es2, [128, 2], F32, "par")
                S.dma("sp", par, par_d)
                selg = self.sb(es2, [128, 64], F32, "selg")
                S.dma("sp", selg, selg_d)
                Cn = self.sb(es2, [128, 16, 64], F32, "Cn")
                Ls = [self.sb(es2, [128, 128], F32, "Ls%d" % k) for k in range(2)]
                li_ = 0
                for src_d, dstq in ((C_re, CQr), (C_im, CQi)):
                    S.dma("sp", Cn, V(src_d, None))
                    for cg in range(2):
                        ps = self.next_ps()
                        for c8 in range(8):
                            c = cg * 8 + c8
                            L = Ls[li_ % 2]
                            li_ += 1
                            S.ts(L[:, 0:64], Cn[:, c, :], par[:, 0:1], ALU.mult)
                            S.ts(L[:, 64:128], Cn[:, c, :], par[:, 1:2], ALU.mult)
                            S.mm(ps[:, c8 * 64:(c8 + 1) * 64], [(L, selg)])
                        S.copy(dstq[:, :, cg * 8:(cg + 1) * 8].re("p i c -> p c i"),
                               ps.re("p (c i) -> p c i", c=8))
            ZBr = [self.sb(es, [128, 128], F32, "ZBr%d" % k) for k in range(4)]
            ZBi = [self.sb(es, [128, 128], F32, "ZBi%d" % k) for k in range(4)]
            ZCr = [self.sb(es, [128, 128], BF16, "ZCr%d" % k) for k in range(4)]
            ZCi = [self.sb(es, [128, 128], BF16, "ZCi%d" % k) for k in range(4)]
            LBr = [self.sb(es, [128, 128], BF16, "LBr%d" % k) for k in range(4)]
            LBi = [self.sb(es, [128, 128], BF16, "LBi%d" % k) for k in range(4)]
            for k in range(4):
                for z in (ZBr[k], ZBi[k], ZCr[k], ZCi[k]):
                    S.memset(z, 0.0, eng="pool")
            Uf = self.sb(es, [128, NTOK], F32, "Uf")
            Ub = self.sb(es, [128, NTOK], BF16, "Ub")
            Er = [self.sb(es, [128, HALF], F32, "Er%d" % k) for k in range(2)]
            Ei = [self.sb(es, [128, HALF], F32, "Ei%d" % k) for k in range(2)]
            Tt = self.sb(es, [128, HALF], F32, "Tt")
            Wr = self.sb(es, [128, HALF], F32, "Wr")
            Wi = self.sb(es, [128, HALF], F32, "Wi")
            Gr = self.sb(es, [128, HALF], F32, "Gr")
            Gi = self.sb(es, [128, HALF], F32, "Gi")
            Hr = [self.sb(es, [128, HALF], BF16, "Hr%d" % k) for k in range(4)]
            Hi = [self.sb(es, [128, HALF], BF16, "Hi%d" % k) for k in range(4)]
            d1 = [self.sb(es, [128, 512], F32, "d1_%d" % k) for k in range(2)]
            d2 = [self.sb(es, [128, 512], F32, "d2_%d" % k) for k in range(2)]
            car = self.sb(es, [128, 64, 2], F32, "car")
            fin = self.sb(es, [128, 2, 64], F32, "fin")
            fins = self.sb(es, [128, 2, NSMP, 64], F32, "fins")
            tn = self.sb(es, [128, 16], F32, "tn")
            tns = self.sb(es, [128, 8, NSMP], F32, "tns")
            ys = [self.sb(es, [128, 512], F32, "ys%d" % k) for k in range(2)]
            yt = [self.sb(es, [128, 512], F32, "yt%d" % k) for k in range(2)]
            go = [self.sb(es, [128, 512], BF16, "go%d" % k) for k in range(2)]
            nhalf = SEQ // HALF
            ei_ = 0
            yi = 0
            for cc in range(16):
                rows = slice(cc * 128, (cc + 1) * 128)
                S.dma("sp", Uf, uT[rows, :])
                S.copy(Ub, Uf, eng="act")
                for j in range(4):
                    i = cc * 4 + j
                    c0 = 32 * j
                    for (zb, bb_) in ((ZBr[j], bbr), (ZBi[j], bbi)):
                        S.copy(zb[0:64, c0:c0 + 16], bb_[0:64, i, :], eng="pool")
                        S.copy(zb[64:128, c0 + 16:c0 + 32], bb_[64:128, i, :], eng="pool")
                    S.copy(ZCr[j][0:64, c0:c0 + 16], CQr[0:64, i, :], eng="pool")
                    S.copy(ZCr[j][64:128, c0 + 16:c0 + 32], CQr[64:128, i, :], eng="pool")
                    S.ts(ZCi[j][0:64, c0:c0 + 16], CQi[0:64, i, :], -1.0, ALU.mult)
                    S.ts(ZCi[j][64:128, c0 + 16:c0 + 32], CQi[64:128, i, :], -1.0, ALU.mult)
                    ps = self.next_ps()
                    S.transposes([(ps[:, 0:128], ZBr[j]), (ps[:, 128:256], ZBi[j])], self.ident)
                    S.copy(LBr[j], ps[:, 0:128], eng="act")
                    S.copy(LBi[j], ps[:, 128:256], eng="act")
                for hf in range(nhalf + 1):
                    smp = hf == nhalf
                    t0 = hf * HALF
                    n = NST if smp else HALF
                    for j in range(4):
                        i = cc * 4 + j
                        if not smp:
                            er, ei = Er[ei_ % 2], Ei[ei_ % 2]
                            ei_ += 1
                            ear = V(EAr.ap[:, i, :].unsqueeze(2).to_broadcast([128, NA, 64]), EAr.trk)
                            eai = V(EAi.ap[:, i, :].unsqueeze(2).to_broadcast([128, NA, 64]), EAi.trk)
                            ebr = V(EBr.ap[:, i, :].unsqueeze(1).to_broadcast([128, NA, 64]), EBr.trk)
                            ebi = V(EBi.ap[:, i, :].unsqueeze(1).to_broadcast([128, NA, 64]), EBi.trk)
                            e3r = er.re("p (a b) -> p a b", b=64)
                            e3i = ei.re("p (a b) -> p a b", b=64)
                            w3r = Tt.re("p (a b) -> p a b", b=64)
                            S.tt(e3r, ear, ebr, ALU.mult, eng="pool")
                            S.tt(w3r, eai, ebi, ALU.mult, eng="pool")
                            S.tt(e3r, e3r, w3r, ALU.subtract, eng="pool")
                            S.tt(e3i, ear, ebi, ALU.mult, eng="pool")
                            S.tt(w3r, eai, ebr, ALU.mult, eng="pool")
                            S.tt(e3i, e3i, w3r, ALU.add, eng="pool")
                            erv = lambda a, b: er[:, a:b]
                            eiv = lambda a, b: ei[:, a:b]
                        for q0 in range(0, n, 512):
                            nn = min(512, n - q0)
                            psr = self.next_ps()
                            S.mm(psr[:, :nn], [(LBr[j], Ub[:, t0 + q0:t0 + q0 + nn])])
                            psi = self.next_ps()
                            S.mm(psi[:, :nn], [(LBi[j], Ub[:, t0 + q0:t0 + q0 + nn])])
                            a1, a2 = d1[(q0 // 512) % 2], d2[(q0 // 512) % 2]
                            if smp:
                                E_r = V(EBr.ap[:, i, 0:LS].unsqueeze(1).to_broadcast([128, NSMP, LS]), EBr.trk)
                                E_i = V(EBi.ap[:, i, 0:LS].unsqueeze(1).to_broadcast([128, NSMP, LS]), EBi.trk)
                                v3 = lambda x: x[:, :nn].re("p (s t) -> p s t", t=LS)
                            else:
                                E_r, E_i = erv(q0, q0 + nn), eiv(q0, q0 + nn)
                                v3 = lambda x: x[:, :nn]
                            wr_o = v3(Wr[:, q0:q0 + nn]) if not smp else v3(Wr)
                            wi_o = v3(Wi[:, q0:q0 + nn]) if not smp else v3(Wi)
                            S.tt(v3(a1), v3(psr), E_r, ALU.mult)
                            S.tt(v3(a2), v3(psi), E_i, ALU.mult)
                            S.tt(wr_o, v3(a1), v3(a2), ALU.add)
                            S.tt(v3(a1), v3(psi), E_r, ALU.mult)
                            S.tt(v3(a2), v3(psr), E_i, ALU.mult)
                            S.tt(wi_o, v3(a1), v3(a2), ALU.subtract)
                        rb = rho[:, i:i + 1]
                        if not smp:
                            ini_r = 0.0 if hf == 0 else car[:, i, 0:1]
                            ini_i = 0.0 if hf == 0 else car[:, i, 1:2]
                            S.scan(Gr[:, :n], rb.bc([128, n]), Wr[:, :n], ini_r, ALU.mult, ALU.add)
                            S.scan(Gi[:, :n], rb.bc([128, n]), Wi[:, :n], ini_i, ALU.mult, ALU.add)
                            S.tt(Wr[:, :n], er[:, :n], Gr[:, :n], ALU.mult, eng="pool")
                            S.tt(Wi[:, :n], ei[:, :n], Gi[:, :n], ALU.mult, eng="pool")
                            S.tt(Hr[j][:, :n], Wr[:, :n], Wi[:, :n], ALU.subtract, eng="pool")
                            S.tt(Wr[:, :n], er[:, :n], Gi[:, :n], ALU.mult)
                            S.tt(Wi[:, :n], ei[:, :n], Gr[:, :n], ALU.mult)
                            S.tt(Hi[j][:, :n], Wr[:, :n], Wi[:, :n], ALU.add)
                            L_ = n - 1
                            S.tt(tn[:, 0:1], er[:, L_:n], Gr[:, L_:n], ALU.mult)
                            S.tt(tn[:, 1:2], ei[:, L_:n], Gi[:, L_:n], ALU.mult)
                            S.tt(tn[:, 2:3], tn[:, 0:1], tn[:, 1:2], ALU.subtract)
                            S.tt(tn[:, 0:1], er[:, L_:n], Gi[:, L_:n], ALU.mult)
                            S.tt(tn[:, 1:2], ei[:, L_:n], Gr[:, L_:n], ALU.mult)
                            S.tt(tn[:, 3:4], tn[:, 0:1], tn[:, 1:2], ALU.add)
                            if hf == nhalf - 1:
                                S.copy(fin[:, 0, i:i + 1], tn[:, 2:3])
                                S.copy(fin[:, 1, i:i + 1], tn[:, 3:4])
                            else:
                                S.tt(tn[:, 4:5], tn[:, 2:3], e1r[:, i:i + 1], ALU.mult)
                                S.tt(tn[:, 5:6], tn[:, 3:4], e1i[:, i:i + 1], ALU.mult)
                                S.tt(car[:, i, 0:1], tn[:, 4:5], tn[:, 5:6], ALU.subtract)
                                S.tt(tn[:, 4:5], tn[:, 2:3], e1i[:, i:i + 1], ALU.mult)
                                S.tt(tn[:, 5:6], tn[:, 3:4], e1r[:, i:i + 1], ALU.mult)
                                S.tt(car[:, i, 1:2], tn[:, 4:5], tn[:, 5:6], ALU.add)
                        else:
                            h0r_ = h0r[:, :, i]
                            h0i_ = h0i[:, :, i]
                            e1rb = e1r[:, i:i + 1].bc([128, NSMP])
                            e1ib = e1i[:, i:i + 1].bc([128, NSMP])
                            S.tt(tns[:, 0, :], h0r_, e1rb, ALU.mult)
                            S.tt(tns[:, 1, :], h0i_, e1ib, ALU.mult)
                            S.tt(tns[:, 2, :], tns[:, 0, :], tns[:, 1, :], ALU.subtract)
                            S.tt(tns[:, 0, :], h0r_, e1ib, ALU.mult)
                            S.tt(tns[:, 1, :], h0i_, e1rb, ALU.mult)
                            S.tt(tns[:, 3, :], tns[:, 0, :], tns[:, 1, :], ALU.add)
                            for s_ in range(NSMP):
                                sl = slice(s_ * LS, (s_ + 1) * LS)
                                S.scan(Gr[:, sl], rb.bc([128, LS]), Wr[:, sl], tns[:, 2, s_:s_ + 1], ALU.mult, ALU.add)
                                S.scan(Gi[:, sl], rb.bc([128, LS]), Wi[:, sl], tns[:, 3, s_:s_ + 1], ALU.mult, ALU.add)
                            g3r = Gr[:, :n].re("p (s t) -> p s t", t=LS)
                            g3i = Gi[:, :n].re("p (s t) -> p s t", t=LS)
                            w3a = Wr[:, :n].re("p (s t) -> p s t", t=LS)
                            w3b = Wi[:, :n].re("p (s t) -> p s t", t=LS)
                            hro = Hr[j][:, :n].re("p (s t) -> p s t", t=LS)
                            hio = Hi[j][:, :n].re("p (s t) -> p s t", t=LS)
                            S.tt(w3a, E_r, g3r, ALU.mult)
                            S.tt(w3b, E_i, g3i, ALU.mult)
                            S.tt(hro, w3a, w3b, ALU.subtract)
                            S.tt(w3a, E_r, g3i, ALU.mult)
                            S.tt(w3b, E_i, g3r, ALU.mult)
                            S.tt(hio, w3a, w3b, ALU.add)
                            er7 = EBr[:, i, LS - 1:LS].bc([128, NSMP])
                            ei7 = EBi[:, i, LS - 1:LS].bc([128, NSMP])
                            S.tt(tns[:, 4, :], er7, g3r[:, :, LS - 1], ALU.mult)
                            S.tt(tns[:, 5, :], ei7, g3i[:, :, LS - 1], ALU.mult)
                            S.tt(fins[:, 0, :, i], tns[:, 4, :], tns[:, 5, :], ALU.subtract)
                            S.tt(tns[:, 4, :], er7, g3i[:, :, LS - 1], ALU.mult)
                            S.tt(tns[:, 5, :], ei7, g3r[:, :, LS - 1], ALU.mult)
                            S.tt(fins[:, 1, :, i], tns[:, 4, :], tns[:, 5, :], ALU.add)
                    for q0 in range(0, n, 512):
                        nn = min(512, n - q0)
                        ps = self.next_ps()
                        pairs = []
                        for j in range(4):
                            pairs.append((ZCr[j], Hr[j][:, q0:q0 + nn]))
                            pairs.append((ZCi[j], Hi[j][:, q0:q0 + nn]))
                        S.mm(ps[:, :nn], pairs)
                        y_, y2, g_ = ys[yi % 2], yt[yi % 2], go[yi % 2]
                        yi += 1
                        S.stt(y_[:, :nn], Uf[:, t0 + q0:t0 + q0 + nn], dcol[:, cc:cc + 1], ps[:, :nn], ALU.mult, ALU.add)
                        S.act(y2[:, :nn], y_[:, :nn], AF.Square)
                        S.ts(y2[:, :nn], y2[:, :nn], 0.044715, ALU.mult, 1.0, ALU.add)
                        S.tt(y2[:, :nn], y2[:, :nn], y_[:, :nn], ALU.mult)
                        S.act(y2[:, :nn], y2[:, :nn], AF.Sigmoid, scale=1.5957691216057308)
                        S.tt(g_[:, :nn], y_[:, :nn], y2[:, :nn], ALU.mult)
                        S.dma("sp", gT[rows, t0 + q0:t0 + q0 + nn], g_[:, :nn])
            S.dma("sp", V(qi(o_re_p.ap), o_re_p.trk), fin[:, 0, :], slow=True)
            S.dma("sp", V(qi(o_im_p.ap), o_im_p.trk), fin[:, 1, :], slow=True)
            for s_ in range(NSMP):
                S.dma("sp", V(qi(o_re_s.ap[s_]), o_re_s.trk), fins[:, 0, s_, :], slow=True)
                S.dma("sp", V(qi(o_im_s.ap[s_]), o_im_s.trk), fins[:, 1, s_, :], slow=True)

        with self.phase() as es:
            wg = self.sb(es, [128, 16, E], BF16, "wglu")
            wv = w_glu.rearrange("(kc p) n -> p kc n", p=128)
            for k in range(8):
                S.dma("pool", wg[:, 2 * k:2 * k + 2, :], wv[:, 2 * k:2 * k + 2, :])
            gb = [self.sb(es, [128, 16, 512], BF16, "gb%d" % k) for k in range(2)]
            zb = [self.sb(es, [128, 512], BF16, "szb%d" % k) for k in range(3)]
            sg = [self.sb(es, [128, 512], F32, "ssg%d" % k) for k in range(2)]
            hb = [self.sb(es, [128, 512], BF16, "shb%d" % k) for k in range(3)]
            gv = gT.re("(kc p) t -> p kc t", p=128)
            zi = 0
            for tt in range((NTOK + 511) // 512):
                ntk = min(512, NTOK - tt * 512)
                tsl = slice(tt * 512, tt * 512 + ntk)
                g = gb[tt % 2]
                S.dma("sp", g[:, :, :ntk], gv[:, :, tsl])
                for nb_ in range(16):
                    ps = self.next_ps()
                    S.mm(ps[:, :ntk], [(wg[:, kc, nb_ * 128:(nb_ + 1) * 128], g[:, kc, :ntk]) for kc in range(16)])
                    z, sg_, h = zb[zi % 3], sg[zi % 2], hb[zi % 3]
                    zi += 1
                    S.dma("sp", z[:, :ntk], zT[nb_ * 128:(nb_ + 1) * 128, tsl])
                    S.act(sg_[:, :ntk], ps[:, :ntk], AF.Sigmoid)
                    S.act(z[:, :ntk], z[:, :ntk], AF.Silu)
                    S.tt(sg_[:, :ntk], sg_[:, :ntk], g[:, nb_, :ntk], ALU.mult)
                    S.tt(h[:, :ntk], sg_[:, :ntk], z[:, :ntk], ALU.mult)
                    S.dma("sp", self.hgt[nb_ * 128:(nb_ + 1) * 128, tsl], h[:, :ntk])

    def layer_attn(self, cur, g_row):
        nc, S = self.nc, self.S
        SEQ, NTOK = self.SEQ, self.NTOK
        PAT = ((128, 1), (512, 4), (2048, 16))
        w_in = self.din("c_w_in", [D, 7168])
        kc_in = [self.din("cache_dil_k%d" % (g + 1), [NSMP, PAT[g][0], 512]) for g in range(3)]
        vc_in = self.din("cache_dil_v", [NSMP, 2048, E])
        mk = self.din("c_attn_mask", [128, 256], BF16)
        smask_d = self.din("c_smask", [128, 17, 192])
        kp = [self.dout("k%d_p" % (g + 1), [min(PAT[g][0], SEQ), 512]) for g in range(3)]
        ks = [self.dout("k%d_s" % (g + 1), [NSMP, PAT[g][0], 512]) for g in range(3)]
        vp = self.dout("v_p", [min(2048, SEQ), E])
        vs = self.dout("v_s", [NSMP, 2048, E])
        qT = self.dscr("at_qT", [1536, NTOK], BF16)
        kT = self.dscr("at_kT", [1536, NTOK], BF16)
        zT = self.dscr("at_zT", [E, NTOK], BF16)
        ktok = self.dscr("at_ktok", [NTOK, 1536], F32)
        vtok = self.dscr("at_vtok", [NTOK, E], F32)
        import os
        nobulk = os.environ.get("ATT_NOBULK") == "1"
        gset = [int(c) for c in os.environ.get("ATT_G", "012")]
        nheads = int(os.environ.get("ATT_HEADS", "8"))
        for s in range(0 if nobulk else NSMP):
            S.dma("act", vs[s, 0:2048 - LS, :], V(vc_in[s, LS:2048, :], None))
            for g in range(3):
                W = PAT[g][0]
                S.dma("act", ks[g][s, 0:W - LS, :], V(kc_in[g][s, LS:W, :], None))
        with self.phase() as es:
            XNT = self.alloc_xnt(es)
            self.phase_norm(cur, g_row, XNT)
            self.inproj_fm(w_in, XNT, [(0, 1536, qT, BF16), (1536, 1536, kT, BF16), (5120, E, zT, BF16)], es)
            self.inproj_tm(w_in, XNT, 1536, 1536, ktok, es)
            self.inproj_tm(w_in, XNT, 3072, E, vtok, es)
        for g in range(3):
            W = min(PAT[g][0], SEQ)
            S.dma("sp", kp[g], ktok[SEQ - W:SEQ, g * 512:(g + 1) * 512])
            for s in range(NSMP):
                S.dma("sp", ks[g][s, PAT[g][0] - LS:PAT[g][0], :],
                      ktok[SEQ + s * LS:SEQ + (s + 1) * LS, g * 512:(g + 1) * 512])
        Wv = min(2048, SEQ)
        S.dma("sp", vp, vtok[SEQ - Wv:SEQ, :])
        for s in range(NSMP):
            S.dma("sp", vs[s, 2048 - LS:2048, :], vtok[SEQ + s * LS:SEQ + (s + 1) * LS, :])

        import os
        stop = int(os.environ.get("ATT_STOP", "9"))
        if stop <= 1:
            return
        with self.phase() as es:
            mask = self.sb(es, [128, 256], BF16, "amask")
            S.dma("sp", mask, mk)
            onesb = self.sb(es, [128, 128], BF16, "onesb")
            S.memset(onesb, 1.0)
            qh = self.sb(es, [64, 3, SEQ], BF16, "qh")
            kh = self.sb(es, [64, 3, SEQ], BF16, "kh")
            nblk = SEQ // 128
            vh = [self.sb(es, [128, nblk, 256], BF16, "vh%d" % g) for g in range(3)]
            num = self.sb(es, [128, 2, SEQ], F32, "anum")
            den = self.sb(es, [128, SEQ], F32, "aden")
            ptb = [self.sb(es, [128, 256], BF16, "pt%d" % i) for i in range(3)]
            zb = [self.sb(es, [128, SEQ], BF16, "azb%d" % i) for i in range(2)]
            pti = 0
            for h in range(nheads):
                for g in range(3):
                    r0 = g * 512 + h * 64
                    S.dma("sp", qh[:, g, :], qT[r0:r0 + 64, 0:SEQ])
                    S.dma("sp", kh[:, g, :], kT[r0:r0 + 64, 0:SEQ])
                    d = PAT[g][1]
                    src = vtok[0:SEQ, h * 256:(h + 1) * 256].re("(kb i r) c -> i r kb c", i=128, r=d)
                    nb = SEQ // (128 * d)
                    for r in range(d):
                        S.dma("pool", vh[g][:, r * nb:(r + 1) * nb, :], src[:, r])
                for g in gset:
                    d = PAT[g][1]
                    nb = SEQ // (128 * d)
                    for r in range(d):
                        for qb in range(nb):
                            def tok(b):
                                st = r + d * 128 * b
                                return slice(st, st + d * 127 + 1, d)
                            ps_s = self.next_ps()
                            S.mm(ps_s[:, 0:128], [(kh[:, g, tok(qb)], qh[:, g, tok(qb)]),
                                                  (self.identb, mask[:, 0:128])])
                            ncol = 128
                            if qb > 0:
                                S.mm(ps_s[:, 128:256], [(kh[:, g, tok(qb - 1)], qh[:, g, tok(qb)]),
                                                        (self.identb, mask[:, 128:256])])
                                ncol = 256
                            pt = ptb[pti % 3]
                            pti += 1
                            asub = int(os.environ.get("ATT_SUB", "9"))
                            if asub <= 0:
                                continue
                            S.act(pt[:, 0:ncol], ps_s[:, 0:ncol], AF.Exp, scale=0.125)
                            if asub <= 1:
                                continue
                            ps_o = self.next_ps()
                            blk = r * nb + qb
                            for dvb in range(3):
                                pairs = []
                                for half in range(2 if qb > 0 else 1):
                                    l = (vh[g][:, blk - half, dvb * 128:(dvb + 1) * 128] if dvb < 2 else onesb)
                                    pairs.append((l, pt[:, half * 128:(half + 1) * 128]))
                                S.mm(ps_o[:, dvb * 128:(dvb + 1) * 128], pairs)
                            if asub <= 2:
                                continue
                            nv = num[:, :, tok(qb)]
                            dv_ = den[:, tok(qb)]
                            pn = ps_o[:, 0:256].re("p (a t) -> p a t", a=2)
                            if g == gset[0]:
                                S.copy(nv, pn, eng="act")
                                S.copy(dv_, ps_o[:, 256:384], eng="dve")
                            else:
                                S.tt(nv, nv, pn, ALU.add)
                                S.tt(dv_, dv_, ps_o[:, 256:384], ALU.add)
                if int(os.environ.get("ATT_SUB", "9")) < 9:
                    continue
                S.recip(den, den)
                for dvb in range(2):
                    z = zb[dvb]
                    row = h * 256 + dvb * 128
                    S.dma("sp", z, zT[row:row + 128, 0:SEQ])
                    S.act(z, z, AF.Silu)
                    S.tt(num[:, dvb, :], num[:, dvb, :], den, ALU.mult)
                    S.tt(z, z, num[:, dvb, :], ALU.mult)
                    S.dma("sp", self.hgt[row:row + 128, 0:SEQ], z)

        if stop <= 2:
            return
        with self.phase() as es:
            smask = self.sb(es, [128, 17, 192], F32, "smask")
            S.dma("sp", smask, smask_d)
            onesb = self.sb(es, [128, 128], BF16, "onesb2")
            S.memset(onesb, 1.0)
            qs = self.sb(es, [64, 24, NST], BF16, "qs")
            kn = self.sb(es, [64, 24, NST], BF16, "kn")
            S.dma("sp", qs, qT[:, SEQ:SEQ + NST].re("(c p) t -> p c t", p=64), slow=True)
            S.dma("sp", kn, kT[:, SEQ:SEQ + NST].re("(c p) t -> p c t", p=64), slow=True)
            kblk = [self.sb(es, [128, 512], F32, "kblk%d" % i) for i in range(4)]
            kTc = [self.sb(es, [64, 8, 128], BF16, "kTc%d" % i) for i in range(4)]
            vblk = [self.sb(es, [128, E], BF16, "vblk%d" % i) for i in range(3)]
            pe_ = [self.sb(es, [128, 192], F32, "pe%d" % i) for i in range(2)]
            pj = [self.sb(es, [128, 64], BF16, "pj%d" % i) for i in range(2)]
            acc = self.sb(es, [128, 192], F32, "sacc")
            zs = self.sb(es, [128, 16, NST], BF16, "zs")
            hs = self.sb(es, [128, 16, NST], BF16, "hs")
            S.dma("sp", zs, zT[:, SEQ:SEQ + NST].re("(c p) t -> p c t", p=128), slow=True)
            S.act(zs, zs, AF.Silu)
            ki = 0
            bi = 0
            for s in range(NSMP):
                for kb in range(17):
                    nk = 128 if kb < 16 else LS
                    glo = 2 if kb < 12 else (1 if kb < 15 else 0)
                    vb = vblk[bi % 3]
                    if kb < 16:
                        S.dma("pool", vb, V(vc_in[s, kb * 128:(kb + 1) * 128, :], None))
                    else:
                        S.dma("pool", vb[:LS], vtok[SEQ + s * LS:SEQ + (s + 1) * LS, :])
                    ps_s = self.next_ps()
                    for g in range(glo, 3):
                        if kb < 16:
                            W = PAT[g][0]
                            kbl = kb - (16 - W // 128)
                            kk = kblk[ki % 4]
                            kt = kTc[ki % 4]
                            ki += 1
                            S.dma("sp", kk, V(kc_in[g][s, kbl * 128:(kbl + 1) * 128, :], None))
                            for hq in range(2):
                                pst = self.next_ps()
                                S.transposes([(pst[0:64, a * 128:(a + 1) * 128],
                                               kk[:, (hq * 4 + a) * 64:(hq * 4 + a + 1) * 64]) for a in range(4)],
                                             self.ident)
                                S.copy(kt[:, hq * 4:hq * 4 + 4, :], pst[0:64, :].re("p (a k) -> p a k", a=4),
                                       eng=self.evac_eng())
                        for h in range(8):
                            if kb < 16:
                                l = kt[:, h, :]
                            else:
                                l = kn[:, g * 8 + h, s * LS:(s + 1) * LS]
                            rr = qs[:, g * 8 + h, s * LS:(s + 1) * LS]
                            c = (g * 8 + h) * 8
                            S.mm(ps_s[:nk, c:c + 8], [(l, rr)])
                    c0 = glo * 64
                    pe = pe_[bi % 2]
                    S.act(pe[:nk, c0:192], ps_s[:nk, c0:192], AF.Exp, scale=0.125)
                    S.tt(pe[:nk, c0:192], pe[:nk, c0:192], smask[:nk, kb, c0:192], ALU.mult)
                    p = pj[bi % 2]
                    if glo == 2:
                        S.copy(p[:nk], pe[:nk, 128:192])
                    elif glo == 1:
                        S.tt(p[:nk], pe[:nk, 64:128], pe[:nk, 128:192], ALU.add)
                    else:
                        S.tt(pe[:nk, 0:64], pe[:nk, 0:64], pe[:nk, 64:128], ALU.add)
                        S.tt(p[:nk], pe[:nk, 0:64], pe[:nk, 128:192], ALU.add)
                    ps_o = self.next_ps()
                    for c in range(16):
                        h = c // 2
                        S.mm(ps_o[:, c * 8:(c + 1) * 8], [(vb[:nk, c * 128:(c + 1) * 128], p[:nk, h * 8:(h + 1) * 8])])
                    S.mm(ps_o[:, 128:192], [(onesb[:nk, :], p[:nk, :])])
                    if kb == 0:
                        S.copy(acc, ps_o[:, 0:192])
                    else:
                        S.tt(acc, acc, ps_o[:, 0:192], ALU.add)
                    bi += 1
                S.recip(acc[:, 128:192], acc[:, 128:192])
                av = acc[:, 0:128].re("p (h b q) -> p h b q", h=8, b=2)
                dvv = acc[:, 128:192].re("p (h q) -> p h q", h=8)
                for b_ in range(2):
                    S.tt(av[:, :, b_, :], av[:, :, b_, :], dvv, ALU.mult)
                S.tt(hs[:, :, s * LS:(s + 1) * LS], zs[:, :, s * LS:(s + 1) * LS],
                     acc[:, 0:128].re("p (c q) -> p c q", q=LS), ALU.mult)
            S.dma("sp", self.hgt[:, SEQ:SEQ + NST].re("(c p) t -> p c t", p=128), hs, slow=True)

    def alloc_xnt(self, es):
        nt512 = (self.NTOK + 511) // 512
        return [self.sb(es, [128, 8, 512], BF16, "xnt%d" % i) for i in range(nt512)]

    def layer_pool(self, cur, g_row):
        nc, S = self.nc, self.S
        SEQ, NTOK = self.SEQ, self.NTOK
        w_in = self.din("d_w_in", [D, 2 * E])
        w_grp = self.din("d_w_grp", [4, 512, 512])
        scale = self.din("d_scale", [E])
        st_in = self.din("state_pool", [NSMP, 15, E])
        invc = self.din("c_pool_inv", [4, 16])
        pool_p = self.dout("pool_p", [15, E])
        pool_s = self.dout("pool_s", [NSMP, 15, E])
        uT = self.dscr("pl_uT", [E, NTOK], F32)
        zT = self.dscr("pl_zT", [E, NTOK], BF16)
        with self.phase() as es:
            XNT = self.alloc_xnt(es)
            self.phase_norm(cur, g_row, XNT)
            self.dump("dbg_xnt0", XNT[0], [128, 8, 512], BF16)
            self.dump("dbg_xnt1", XNT[1], [128, 8, 512], BF16)
            self.inproj_fm(w_in, XNT, [(0, E, uT, F32), (E, E, zT, BF16)], es)
            S.barrier()
        with self.phase() as es:
            pref = self.sb(es, [128, NSMP, 16, 15], F32, "pref")
            stt_ = [self.sb(es, [15, E], F32, "pst%d" % i) for i in range(2)]
            for s in range(NSMP):
                t = stt_[s % 2]
                S.dma("sp", t, V(st_in[s], None))
                ps = self.next_ps()
                S.transposes([(ps[:, cc * 15:(cc + 1) * 15], t[0:15, cc * 128:(cc + 1) * 128]) for cc in range(16)],
                             self.ident)
                S.copy(pref[:, s], ps[:, 0:240].re("p (c r) -> p c r", r=15), eng=self.evac_eng())
            sc_col = self.sb(es, [128, 16], F32, "sccol")
            S.dma("sp", sc_col, V(scale.rearrange("(c p) -> p c", p=128), None), slow=True)
            icb = self.sb(es, [128, 4, 16], F32, "icb")
            S.dma("sp", icb.re("p g t -> p (g t)"), invc.rearrange("g t -> (g t)").partition_broadcast(128))
            ext = [self.sb(es, [128, 16 + SEQ], F32, "ext%d" % i) for i in range(2)]
            wk = [self.sb(es, [128, 16 + SEQ], F32, "wk%d" % i) for i in range(2)]
            exs = [self.sb(es, [128, NSMP, 24], F32, "exs%d" % i) for i in range(2)]
            wks = [self.sb(es, [128, NSMP, 24], F32, "wks%d" % i) for i in range(2)]
            mix = [self.sb(es, [128, NTOK], BF16, "mix%d" % i) for i in range(8)]
            pst = [self.sb(es, [15, 128], F32, "pso%d" % i) for i in range(4)]
            pi = 0
            wg = [self.sb(es, [128, 4, 512], BF16, "wg%d" % i) for i in range(2)]
            zb = [self.sb(es, [128, 512], BF16, "zb%d" % i) for i in range(3)]
            hb = [self.sb(es, [128, 512], BF16, "hb%d" % i) for i in range(3)]
            zi = 0
            for g in range(4):
                nsteps = g + 1
                w = 2 ** (g + 1)
                wgt = wg[g % 2]
                S.dma("pool", wgt, V(w_grp[g].rearrange("(kc p) n -> p kc n", p=128), None))
                for c4 in range(4):
                    cc = g * 4 + c4
                    ex = ext[cc % 2]
                    m = mix[(g % 2) * 4 + c4]
                    S.memset(ex[:, 0:16], 0.0, eng="pool")
                    S.dma("sp", ex[:, 16:16 + SEQ], uT[cc * 128:(cc + 1) * 128, 0:SEQ])
                    a, b = ex, wk[0]
                    sh = 1
                    for stp in range(nsteps):
                        L = 16 + SEQ
                        S.tt(b[:, sh:L], a[:, sh:L], a[:, 0:L - sh], ALU.add)
                        if sh > 1 or True:
                            pass
                        a, b = b, (wk[1] if b is wk[0] else wk[0])
                        sh *= 2
                    S.stt(m[:, 0:SEQ], a[:, 16:16 + SEQ], 1.0 / w, ex[:, 16:16 + SEQ], ALU.mult, ALU.subtract)
                    fx = b
                    S.tt(fx[:, 0:16], a[:, 16:32], icb[:, g, :], ALU.mult)
                    S.tt(m[:, 0:16], fx[:, 0:16], ex[:, 16:32], ALU.subtract)
                    ps = self.next_ps()
                    S.transposes([(ps[0:15, 0:128], ex[:, 16 + SEQ - 15:16 + SEQ])], self.ident)
                    po = pst[pi % 4]
                    pi += 1
                    S.copy(po, ps[0:15, 0:128], eng=self.evac_eng())
                    S.dma("sp", pool_p[:, cc * 128:(cc + 1) * 128], po)
                    xs = exs[cc % 2]
                    S.memset(xs[:, :, 0:1], 0.0, eng="pool")
                    S.copy(xs[:, :, 1:16], pref[:, :, cc, :], eng="pool")
                    S.dma("sp", xs[:, :, 16:24], uT[cc * 128:(cc + 1) * 128, SEQ:SEQ + NST].re("p (s t) -> p s t", t=LS))
                    a2, b2 = xs, wks[0]
                    sh = 1
                    for stp in range(nsteps):
                        S.tt(b2[:, :, sh:24], a2[:, :, sh:24], a2[:, :, 0:24 - sh], ALU.add)
                        a2, b2 = b2, (wks[1] if b2 is wks[0] else wks[0])
                        sh *= 2
                    S.stt(m[:, SEQ:SEQ + NST].re("p (s t) -> p s t", t=LS), a2[:, :, 16:24], 1.0 / w,
                          xs[:, :, 16:24], ALU.mult, ALU.subtract)
                    for s in range(NSMP):
                        ps = self.next_ps()
                        S.transposes([(ps[0:15, 0:128], xs[:, s, 9:24])], self.ident)
                        po = pst[pi % 4]
                        pi += 1
                        S.copy(po, ps[0:15, 0:128], eng=self.evac_eng())
                        S.dma("sp", pool_s[s, :, cc * 128:(cc + 1) * 128], po)
                for j in range(4):
                    col = g * 4 + j
                    for tt in range((NTOK + 511) // 512):
                        ntk = min(512, NTOK - tt * 512)
                        ps = self.next_ps()
                        S.mm(ps[:, :ntk], [(wgt[:, kc, j * 128:(j + 1) * 128],
                                            mix[(g % 2) * 4 + kc][:, tt * 512:tt * 512 + ntk]) for kc in range(4)])
                        z = zb[zi % 3]
                        h = hb[zi % 3]
                        zi += 1
                        S.dma("sp", z[:, :ntk], zT[col * 128:(col + 1) * 128, tt * 512:tt * 512 + ntk])
                        S.act(z[:, :ntk], z[:, :ntk], AF.Silu)
                        S.stt(h[:, :ntk], ps[:, :ntk], sc_col[:, col:col + 1], z[:, :ntk], ALU.mult, ALU.mult)
                        S.dma("sp", self.hgt[col * 128:(col + 1) * 128, tt * 512:tt * 512 + ntk], h[:, :ntk])
            S.barrier()


def host_consts():
    c = {}
    c["c_ident"] = np.eye(128, dtype=np.float32)
    inv = np.zeros((4, 16), np.float32)
    for g in range(4):
        w = 2 ** (g + 1)
        for t in range(16):
            inv[g, t] = 1.0 / min(t + 1, w)
    c["c_pool_inv"] = inv
    import ml_dtypes
    m = np.zeros((128, 256), np.float32)
    kk = np.arange(128)[:, None]
    qq = np.arange(128)[None, :]
    m[:, 0:128] = np.where(kk <= qq, 0.0, NEG)
    m[:, 128:256] = np.where(kk >= qq, 0.0, NEG)
    c["c_attn_mask"] = m.astype(ml_dtypes.bfloat16)
    sm = np.zeros((128, 17, 3, 8, 8), np.float32)
    PAT = ((128, 1), (512, 4), (2048, 16))
    for kb in range(17):
        for r in range(128 if kb < 16 else LS):
            for g, (W, d) in enumerate(PAT):
                for i in range(LS):
                    dist = (2048 + i - (128 * kb + r)) if kb < 16 else (i - r)
                    ok = dist >= 0 and dist % d == 0 and dist <= W
                    if ok:
                        sm[r, kb, g, :, i] = 1.0
    c["c_smask"] = sm.reshape(128, 17, 192)
    sel = np.zeros((4, 4, 128), np.float32)
    for h in range(4):
        sel[h, h, :] = 1.0
    c["c_sel4"] = sel.reshape(4, 512)
    c["c_ident4"] = np.eye(4, dtype=np.float32)
    par = np.zeros((128, 2), np.float32)
    par[0::2, 0] = 1.0
    par[1::2, 1] = 1.0
    c["c_par"] = par
    sg = np.zeros((128, 64), np.float32)
    sg[np.arange(128), np.arange(128) // 2] = 1.0
    c["c_selg"] = sg
    c["c_iota64"] = np.arange(64, dtype=np.float32)
    c["c_maskneg"] = np.where(np.arange(128)[:, None] <= np.arange(128)[None, :], 0.0, -1e9).astype(np.float32)
    return c


_CACHE = {}


def run(inputs, seq, kinds, debug=False):
    key = (seq, tuple(kinds))
    if key not in _CACHE:
        b = Builder(seq, kinds)
        b.debug = debug
        b.build()
        _CACHE[key] = b
    b = _CACHE[key]
    consts = host_consts()
    in_maps = []
    for c in range(8):
        pb = c // 2
        m = {}
        for name in b.inputs:
            if name in consts:
                m[name] = consts[name]
            elif name == "xin":
                m[name] = np.ascontiguousarray(np.concatenate(
                    [inputs["x_prompt"][pb], inputs["x_sample"][NSMP * c:NSMP * (c + 1)].reshape(NST, D)], axis=0))
            elif name.startswith("state_") or name.startswith("cache_"):
                a = inputs[name][0, NSMP * c:NSMP * (c + 1)]
                if name.startswith("cache_"):
                    a = a.reshape(a.shape[0], a.shape[1], -1)
                m[name] = np.ascontiguousarray(a)
            elif name in ("norm_g", "final_norm_g"):
                m[name] = np.ascontiguousarray(inputs[name])
            else:
                a = inputs[name]
                m[name] = np.ascontiguousarray(a[0])
        in_maps.append(m)
    res = run_bass_kernel_spmd(b.nc, in_maps, core_ids=list(range(8)))
    return b, res.results


def kernel(**inputs):
    inputs = {k: np.asarray(v) for k, v in inputs.items()}
    b, r = run(inputs, 4096, [0, 1, 2, 3])
    o = assemble(b, r, 4096)
    order = ["y_prompt", "y_sample"]
    for nm in ("mlstm_c", "mlstm_n", "mlstm_m", "mlstm_conv", "s5_re", "s5_im", "k1", "k2", "k3", "v", "pool"):
        order += [nm + "_p", nm + "_s"]
    return tuple(np.ascontiguousarray(o[k], dtype=np.float32) for k in order)


def assemble(b, r, seq):
    out = {}
    out["y_prompt"] = np.stack([r[2 * i]["y"][:seq] for i in range(4)])
    out["y_sample"] = np.concatenate([r[c]["y"][seq:].reshape(NSMP, LS, D) for c in range(8)])
    for nm in b.outputs:
        if nm == "y":
            continue
        if nm.endswith("_p"):
            a = np.stack([r[2 * i][nm] for i in range(4)])[None]
        else:
            a = np.concatenate([r[c][nm] for c in range(8)])[None]
        if nm == "mlstm_m_p":
            a = a.reshape(1, 4, 4)
        elif nm == "mlstm_m_s":
            a = np.concatenate([r[c][nm].T for c in range(8)])[None]
        if nm[0] == "k" and nm[1] in "123":
            a = a.reshape(a.shape[:3] + (8, 64))
        elif nm in ("v_p", "v_s"):
            a = a.reshape(a.shape[:3] + (8, 256))
        out[nm] = a
    return out
```
